# Optimizing a Trainium2 kernel written in Bass

```python
import jax, jax.numpy as jnp
from jax import lax
import numpy as np

D_MODEL = 1024
BATCH = 8
SEQ = 4096
DEPTH = 4

N_MIXERS = 3
GRID_W = 64
D_FF = 2816
NORM_EPS = 1e-6
POOL_WINDOWS = (2, 4, 8, 16)
N_POOL_GROUPS = 4
POOL_GROUP = D_MODEL // N_POOL_GROUPS
N_FOURIER_GROUPS = 4
FOURIER_GROUP = D_MODEL // N_FOURIER_GROUPS
HEAD_DIM = 128
N_Q_HEADS = D_MODEL // HEAD_DIM
N_KV_HEADS = N_Q_HEADS // 4
Q_PER_KV = N_Q_HEADS // N_KV_HEADS
D_Q = N_Q_HEADS * HEAD_DIM
D_KV = N_KV_HEADS * HEAD_DIM
Q_BLOCK = 128
ROPE_THETA = 10000.0

kernel_name = "hybrid_pool_fourier_gqa_macaron_encoder"


def rms_norm(x, gain):
    xf = x.astype(jnp.float32)
    y = xf * lax.rsqrt(jnp.mean(xf * xf, axis=-1, keepdims=True) + NORM_EPS)
    return (y * gain.astype(jnp.float32)).astype(x.dtype)


def swiglu(h, w_gate, w_up, w_down):
    return (jax.nn.silu(h @ w_gate) * (h @ w_up)) @ w_down


def pool_mixer(h, w_grp, b_grp, scale):
    B, S, D = h.shape
    hf = h.astype(jnp.float32)
    csum = jnp.concatenate([jnp.zeros((B, 1, D), jnp.float32), jnp.cumsum(hf, axis=1)], axis=1)
    t = jnp.arange(S)
    outs = []
    for g, w in enumerate(POOL_WINDOWS):
        lo = jnp.clip(t - w // 2, 0, S)
        hi = jnp.clip(t + w // 2, 0, S)
        cg = csum[..., g * POOL_GROUP:(g + 1) * POOL_GROUP]
        seg = jnp.take(cg, hi, axis=1) - jnp.take(cg, lo, axis=1)
        cnt = (hi - lo).astype(jnp.float32)
        outs.append(seg / cnt[None, :, None])
    pooled = (jnp.concatenate(outs, axis=-1) - hf).astype(h.dtype)
    pooled = pooled.reshape(B, S, N_POOL_GROUPS, POOL_GROUP)
    y = jnp.einsum('bsgc,gcd->bsgd', pooled, w_grp) + b_grp
    return y.reshape(B, S, D) * scale


def fourier_mixer(h, w_out, b_out):
    B, S, D = h.shape
    hf = h.astype(jnp.float32).reshape(B, S, N_FOURIER_GROUPS, FOURIER_GROUP)
    f = jnp.fft.fft2(hf, axes=(1, 3), norm='ortho').real
    f = f.astype(h.dtype).reshape(B, S, D)
    return f @ w_out + b_out


def axial_rope_tables(seq_len, dtype):
    rows = seq_len // GRID_W
    row = jnp.repeat(jnp.arange(rows, dtype=jnp.float32), GRID_W)
    col = jnp.tile(jnp.arange(GRID_W, dtype=jnp.float32), rows)
    half = HEAD_DIM // 2
    inv_freq = ROPE_THETA ** (-jnp.arange(0, half, 2, dtype=jnp.float32) / half)
    ang_r = row[:, None] * inv_freq[None, :]
    ang_c = col[:, None] * inv_freq[None, :]
    ang = jnp.concatenate([ang_r, ang_r, ang_c, ang_c], axis=-1)
    return jnp.cos(ang).astype(dtype), jnp.sin(ang).astype(dtype)


def _rotate_half(u):
    u1, u2 = jnp.split(u, 2, axis=-1)
    return jnp.concatenate([-u2, u1], axis=-1)


def apply_axial_rope(x, cos, sin):
    half = HEAD_DIM // 2
    rot = jnp.concatenate([_rotate_half(x[..., :half]), _rotate_half(x[..., half:])], axis=-1)
    return x * cos[:, None, :] + rot * sin[:, None, :]


def gqa_axial_attention(h, w_qkv, q_gain, k_gain, w_o, cos, sin):
    B, S, D = h.shape
    n_blk = S // Q_BLOCK
    qkv = h @ w_qkv
    q = qkv[..., :D_Q].reshape(B, S, N_Q_HEADS, HEAD_DIM)
    k = qkv[..., D_Q:D_Q + D_KV].reshape(B, S, N_KV_HEADS, HEAD_DIM)
    v = qkv[..., D_Q + D_KV:].reshape(B, S, N_KV_HEADS, HEAD_DIM)
    q = apply_axial_rope(rms_norm(q, q_gain), cos, sin) * (HEAD_DIM ** -0.5)
    k = apply_axial_rope(rms_norm(k, k_gain), cos, sin)
    q = q.reshape(B, n_blk, Q_BLOCK, N_KV_HEADS, Q_PER_KV, HEAD_DIM).transpose(1, 0, 2, 3, 4, 5)

    def attend(qb):
        s = jnp.einsum('bqgrd,bkgd->bgrqk', qb, k).astype(jnp.float32)
        p = jax.nn.softmax(s, axis=-1).astype(v.dtype)
        return jnp.einsum('bgrqk,bkgd->bqgrd', p, v)

    o = lax.map(attend, q)
    o = o.transpose(1, 0, 2, 3, 4, 5).reshape(B, S, D_Q)
    return o @ w_o


def setup_inputs(seed: int = 0) -> dict:
    key = jax.random.key(seed)
    ks = jax.random.split(key, 24)
    n_pool = len(range(0, DEPTH, N_MIXERS))
    n_fourier = len(range(1, DEPTH, N_MIXERS))
    n_attn = len(range(2, DEPTH, N_MIXERS))

    def nrm(k, shape, fan_in):
        return jax.random.normal(k, shape, jnp.float32) * (fan_in ** -0.5)

    def gain(k, shape):
        return 1.0 + 0.02 * jax.random.normal(k, shape, jnp.float32)

    def small(k, shape):
        return 0.01 * jax.random.normal(k, shape, jnp.float32)

    return {
        'x': jax.random.normal(ks[0], (BATCH, SEQ, D_MODEL), jnp.float32),
        'ffn1_norm': gain(ks[1], (DEPTH, D_MODEL)),
        'ffn1_w_gate': nrm(ks[2], (DEPTH, D_MODEL, D_FF), D_MODEL),
        'ffn1_w_up': nrm(ks[3], (DEPTH, D_MODEL, D_FF), D_MODEL),
        'ffn1_w_down': nrm(ks[4], (DEPTH, D_FF, D_MODEL), D_FF),
        'mixer_norm': gain(ks[5], (DEPTH, D_MODEL)),
        'ffn2_norm': gain(ks[6], (DEPTH, D_MODEL)),
        'ffn2_w_gate': nrm(ks[7], (DEPTH, D_MODEL, D_FF), D_MODEL),
        'ffn2_w_up': nrm(ks[8], (DEPTH, D_MODEL, D_FF), D_MODEL),
        'ffn2_w_down': nrm(ks[9], (DEPTH, D_FF, D_MODEL), D_FF),
        'pool_w': nrm(ks[10], (n_pool, N_POOL_GROUPS, POOL_GROUP, POOL_GROUP), POOL_GROUP),
        'pool_b': small(ks[11], (n_pool, N_POOL_GROUPS, POOL_GROUP)),
        'pool_scale': gain(ks[12], (n_pool, D_MODEL)),
        'fourier_w': nrm(ks[13], (n_fourier, D_MODEL, D_MODEL), D_MODEL),
        'fourier_b': small(ks[14], (n_fourier, D_MODEL)),
        'attn_w_qkv': nrm(ks[15], (n_attn, D_MODEL, D_Q + 2 * D_KV), D_MODEL),
        'attn_q_norm': gain(ks[16], (n_attn, HEAD_DIM)),
        'attn_k_norm': gain(ks[17], (n_attn, HEAD_DIM)),
        'attn_w_o': nrm(ks[18], (n_attn, D_Q, D_MODEL), D_Q),
        'final_norm': gain(ks[19], (D_MODEL,)),
    }


def reference(x, ffn1_norm, ffn1_w_gate, ffn1_w_up, ffn1_w_down, mixer_norm,
              ffn2_norm, ffn2_w_gate, ffn2_w_up, ffn2_w_down,
              pool_w, pool_b, pool_scale, fourier_w, fourier_b,
              attn_w_qkv, attn_q_norm, attn_k_norm, attn_w_o, final_norm):
    cos, sin = axial_rope_tables(x.shape[1], x.dtype)
    for i in range(DEPTH):
        h = rms_norm(x, ffn1_norm[i])
        x = x + 0.5 * swiglu(h, ffn1_w_gate[i], ffn1_w_up[i], ffn1_w_down[i])
        h = rms_norm(x, mixer_norm[i])
        kind = i % N_MIXERS
        j = i // N_MIXERS
        if kind == 0:
            y = pool_mixer(h, pool_w[j], pool_b[j], pool_scale[j])
        elif kind == 1:
            y = fourier_mixer(h, fourier_w[j], fourier_b[j])
        else:
            y = gqa_axial_attention(h, attn_w_qkv[j], attn_q_norm[j], attn_k_norm[j],
                                    attn_w_o[j], cos, sin)
        x = x + y
        h = rms_norm(x, ffn2_norm[i])
        x = x + 0.5 * swiglu(h, ffn2_w_gate[i], ffn2_w_up[i], ffn2_w_down[i])
    return rms_norm(x, final_norm)
```

```python
import numpy as np
import ml_dtypes
from contextlib import ExitStack
import concourse.bass as bass
import concourse.mybir as mybir
from concourse.bass_utils import run_bass_kernel_spmd

F32 = mybir.dt.float32
BF16 = mybir.dt.bfloat16
AF = mybir.ActivationFunctionType
ALU = mybir.AluOpType

D = 1024
S = 4096
NB = 8
DFF = 2816
NJ = DFF // 128
NC = D // 128
DEPTH = 4
EPS = 1e-6
BF16_NP = ml_dtypes.bfloat16


class Buf:
    __slots__ = ("name", "lw", "rd")

    def __init__(self, name):
        self.name = name
        self.lw = None
        self.rd = []


class Op:
    __slots__ = ("eng", "fn", "deps", "signal", "val", "sem", "is_dma")

    def __init__(self, eng, fn):
        self.eng = eng
        self.fn = fn
        self.deps = []
        self.signal = False
        self.val = None
        self.sem = None
        self.is_dma = False


class Rec:
    CE = ("pe", "act", "dve", "pool")

    def __init__(self, nc, es, tag):
        self.nc = nc
        self.es = es
        self.tag = tag
        self.ops = {e: [] for e in ("pe", "act", "dve", "pool", "sp")}
        self.all_sems = []
        self.esem = {e: self._alloc(f"{tag}_s_{e}") for e in self.CE}
        self.dsem = {}

    def _alloc(self, name):
        h = self.nc.alloc_semaphore(name=name)
        self.all_sems.append(h)
        return h

    def _dma_sem(self, name):
        if name not in self.dsem:
            self.dsem[name] = [self._alloc(f"{self.tag}_d_{name}"), 0]
        return self.dsem[name]

    def _hazards(self, o, reads, writes):
        deps = []
        for b in reads:
            if b.lw is not None:
                deps.append(b.lw)
        for b in writes:
            if b.lw is not None:
                deps.append(b.lw)
            deps.extend(b.rd)
        for d in deps:
            if d is o:
                continue
            if d.eng == "pe" and o.eng == "pe":
                continue
            if not d.is_dma:
                d.signal = True
            o.deps.append(d)
        for b in reads:
            b.rd.append(o)
        for b in writes:
            b.lw = o
            b.rd = []

    def op(self, eng, fn, reads=(), writes=()):
        o = Op(eng, fn)
        self._hazards(o, reads, writes)
        self.ops[eng].append(o)
        return o

    def dma(self, queue, fn, sem, reads=(), writes=()):
        o = Op(queue, fn)
        o.is_dma = True
        s = self._dma_sem(sem)
        s[1] += 16
        o.sem = s[0]
        o.val = s[1]
        self._hazards(o, reads, writes)
        self.ops[queue].append(o)
        return o

    def emit(self, final_waits=()):
        nc = self.nc
        for e in self.CE:
            cnt = 0
            for o in self.ops[e]:
                if o.is_dma:
                    continue
                if o.signal:
                    cnt += 1
                    o.val = cnt
                    o.sem = self.esem[e]
        ops = self.ops

        def run(eng, name, extra=()):
            waited = {}
            for o in ops[name]:
                for d in o.deps:
                    key = id(d.sem)
                    if waited.get(key, 0) >= d.val:
                        continue
                    waited[key] = d.val
                    eng.wait_ge(d.sem, d.val)
                ins = o.fn(eng)
                if o.is_dma:
                    ins.then_inc(o.sem, 16)
                elif o.signal:
                    ins.then_inc(o.sem, 1)
            for d in extra:
                eng.wait_ge(d.sem, d.val)

        with nc.Block() as block:
            @block.tensor
            def _(eng):
                run(eng, "pe")

            @block.scalar
            def _(eng):
                run(eng, "act")

            @block.vector
            def _(eng):
                run(eng, "dve")

            @block.gpsimd
            def _(eng):
                run(eng, "pool")

            @block.sync
            def _(eng):
                run(eng, "sp", final_waits)

        nc.all_engine_barrier()
        nc.clear_and_free_semaphores(self.all_sems)
        nc.all_engine_barrier()


def mm_group(R, out_ap, pairs, reads, bank):
    n = len(pairs)
    last = None
    for i, (l, r) in enumerate(pairs):
        def fn(eng, l=l, r=r, i=i):
            return eng.matmul(out_ap, l, r, start=(i == 0), stop=(i == n - 1))
        if i == 0:
            last = R.op("pe", fn, reads=reads, writes=[bank])
        else:
            last = R.op("pe", fn, reads=reads if i == n - 1 else (), writes=[bank])
    return last


class NormCtx:
    def __init__(self, nc, es, tag, nslots=2, W=512, ss=None, ss_b=None):
        self.ns = nslots
        self.W = W
        self.xs = [es.enter_context(nc.sbuf_tensor(f"{tag}_xs{i}", [128, NC, W], F32)) for i in range(nslots)]
        self.xs_b = [Buf(f"xs{i}") for i in range(nslots)]
        self.sq = es.enter_context(nc.sbuf_tensor(f"{tag}_sq", [128, NC, W], BF16))
        self.sq_b = Buf("sq")
        self.ms = [es.enter_context(nc.sbuf_tensor(f"{tag}_ms{i}", [128, W], F32)) for i in range(2)]
        self.ms_b = [Buf(f"ms{i}") for i in range(2)]
        self.ones = es.enter_context(nc.sbuf_tensor(f"{tag}_ones", [128, 128], BF16))
        self.ones_b = Buf("ones")
        if ss is None:
            nb = (W + 511) // 512
            self.ss = es.enter_context(nc.psum_tensor(f"{tag}_ssps", [128, 512 * nb], F32))
            self.ss_b = Buf("ssps")
        else:
            self.ss, self.ss_b = ss, ss_b
        self.k = 0


def norm_init(R, N):
    R.op("dve", lambda eng: eng.memset(N.ones[:], 1.0), writes=[N.ones_b])


def norm_subtile(R, N, loader, gain_ap, h_out, h_buf, gain_buf):
    i = N.k % N.ns
    mi = N.k % 2
    N.k += 1
    W = N.W
    xs, xb = N.xs[i], N.xs_b[i]
    loader(R, xs, xb, i)
    R.op("act", lambda eng: eng.activation(out=N.sq[:], in_=xs[:], func=AF.Square),
         reads=[xb], writes=[N.sq_b])
    for c0 in range(0, W, 512):
        c1 = min(W, c0 + 512)
        mm_group(R, N.ss[:, c0:c1], [(N.ones[:], N.sq[:, c, c0:c1]) for c in range(NC)],
                 reads=[N.sq_b, N.ones_b], bank=N.ss_b)
    ms, mb = N.ms[mi], N.ms_b[mi]
    R.op("dve", lambda eng: eng.tensor_scalar(out=ms[:], in0=N.ss[:, 0:W], scalar1=1.0 / D, scalar2=EPS,
                                              op0=ALU.mult, op1=ALU.add),
         reads=[N.ss_b], writes=[mb])
    R.op("act", lambda eng: eng.activation(out=ms[:], in_=ms[:], func=AF.Sqrt),
         reads=[mb], writes=[mb])
    R.op("dve", lambda eng: eng.reciprocal(out=ms[:], in_=ms[:]), reads=[mb], writes=[mb])
    for c in range(NC):
        R.op("dve", lambda eng, c=c: eng.scalar_tensor_tensor(
            out=h_out[:, c, :], in0=xs[:, c, :], scalar=gain_ap[:, c:c + 1], in1=ms[:],
            op0=ALU.mult, op1=ALU.mult),
            reads=[xb, mb, gain_buf], writes=[h_buf])
    return xs, xb


def std_loader(src_v, t0):
    def ld(R, xs, xb, i):
        R.dma("sp", lambda eng: eng.dma_start(out=xs[:], in_=src_v[:, :, t0:t0 + 512]),
              f"xs{i}", writes=[xb])
    return ld


def ffn_phase(nc, tag, src, dst, wg, wu, wd, gain, T=1024, final_out=False):
    NS = T // 512
    src_v = src.rearrange("(c p) t -> p c t", p=128)
    dst_v = dst.rearrange("(c p) t -> p c t", p=128)
    with ExitStack() as es:
        R = Rec(nc, es, tag)
        sb = lambda name, shape, dt: es.enter_context(nc.sbuf_tensor(f"{tag}_{name}", shape, dt))
        ps = lambda name: es.enter_context(nc.psum_tensor(f"{tag}_{name}", [128, 512], F32))
        N = NormCtx(nc, es, tag)
        g_sb = sb("gain", [128, NC], F32)
        g_b = Buf("gain")
        hT = sb("hT", [128, NC, T], BF16)
        hT_b = [Buf(f"hT{n}") for n in range(NS)]
        aT = sb("aT", [128, NJ, T], BF16)
        aT_b = [[Buf(f"aT{j}_{n}") for n in range(NS)] for j in range(NJ)]
        NW = 3
        wgb = [sb(f"wg{i}", [128, NC, 128], BF16) for i in range(NW)]
        wub = [sb(f"wu{i}", [128, NC, 128], BF16) for i in range(NW)]
        wgb_b = [Buf(f"wg{i}") for i in range(NW)]
        wub_b = [Buf(f"wu{i}") for i in range(NW)]
        NWD = 2
        wdb = [sb(f"wd{i}", [128, NJ, 128], BF16) for i in range(NWD)]
        wdb_b = [Buf(f"wd{i}") for i in range(NWD)]
        ssb = [sb(f"s{i}", [128, 512], F32) for i in range(2)]
        ssb_b = [Buf(f"s{i}") for i in range(2)]
        NX = 3
        xr = [sb(f"xr{i}", [128, 512], F32) for i in range(NX)]
        xr_b = [Buf(f"xr{i}") for i in range(NX)]
        xo = [sb(f"xo{i}", [128, 512], F32) for i in range(NX)]
        xo_b = [Buf(f"xo{i}") for i in range(NX)]
        gps = [ps(f"gps{i}") for i in range(2)]
        ups = [ps(f"ups{i}") for i in range(2)]
        yps = [ps(f"yps{i}") for i in range(2)]
        gps_b = [Buf(f"gps{i}") for i in range(2)]
        ups_b = [Buf(f"ups{i}") for i in range(2)]
        yps_b = [Buf(f"yps{i}") for i in range(2)]

        norm_init(R, N)
        R.dma("sp", lambda eng: eng.dma_start(out=g_sb[:], in_=gain), "gain", writes=[g_b])
        stores = []
        kw = 0
        kd = 0
        kb = 0
        ky = 0
        kx = 0
        for st in range(S // T):
            tb = st * T
            for n in range(NS):
                norm_subtile(R, N, std_loader(src_v, tb + n * 512), g_sb,
                             hT[:, :, n * 512:(n + 1) * 512], hT_b[n], g_b)
            for j in range(NJ):
                wi = kw % NW
                kw += 1
                R.dma("pool", lambda eng, wi=wi, j=j: eng.dma_start(
                    out=wgb[wi][:].rearrange("p c m -> p (c m)"),
                    in_=wg[j].rearrange("p c m -> p (c m)")), f"wg{wi}", writes=[wgb_b[wi]])
                R.dma("pool", lambda eng, wi=wi, j=j: eng.dma_start(
                    out=wub[wi][:].rearrange("p c m -> p (c m)"),
                    in_=wu[j].rearrange("p c m -> p (c m)")), f"wu{wi}", writes=[wub_b[wi]])
                for n in range(NS):
                    bi = kb % 2
                    kb += 1
                    tsl = slice(n * 512, (n + 1) * 512)
                    mm_group(R, gps[bi][:], [(wgb[wi][:, c, :], hT[:, c, tsl]) for c in range(NC)],
                             reads=[wgb_b[wi], hT_b[n]], bank=gps_b[bi])
                    mm_group(R, ups[bi][:], [(wub[wi][:, c, :], hT[:, c, tsl]) for c in range(NC)],
                             reads=[wub_b[wi], hT_b[n]], bank=ups_b[bi])
                    R.op("act", lambda eng, bi=bi: eng.activation(out=ssb[bi][:], in_=gps[bi][:], func=AF.Silu),
                         reads=[gps_b[bi]], writes=[ssb_b[bi]])
                    R.op("dve", lambda eng, bi=bi, j=j, tsl=tsl: eng.tensor_tensor(
                        out=aT[:, j, tsl], in0=ssb[bi][:], in1=ups[bi][:], op=ALU.mult),
                        reads=[ssb_b[bi], ups_b[bi]], writes=[aT_b[j][n]])
            for m in range(NC):
                di = kd % NWD
                kd += 1
                R.dma("pool", lambda eng, di=di, m=m: eng.dma_start(
                    out=wdb[di][:].rearrange("p (a b) q -> p a (b q)", a=2),
                    in_=wd[m].rearrange("p (a b) q -> p a (b q)", a=2)), f"wd{di}", writes=[wdb_b[di]])
                for n in range(NS):
                    yi = ky % 2
                    ky += 1
                    xi = kx % NX
                    kx += 1
                    t0 = tb + n * 512
                    tsl = slice(n * 512, (n + 1) * 512)
                    R.dma("sp", lambda eng, xi=xi, m=m, t0=t0: eng.dma_start(
                        out=xr[xi][:], in_=src_v[:, m, t0:t0 + 512]), f"xr{xi}", writes=[xr_b[xi]])
                    mm_group(R, yps[yi][:], [(wdb[di][:, j, :], aT[:, j, tsl]) for j in range(NJ)],
                             reads=[wdb_b[di]] + [aT_b[j][n] for j in range(NJ)], bank=yps_b[yi])
                    R.op("dve", lambda eng, yi=yi, xi=xi: eng.scalar_tensor_tensor(
                        out=xo[xi][:], in0=yps[yi][:], scalar=0.5, in1=xr[xi][:],
                        op0=ALU.mult, op1=ALU.add),
                        reads=[yps_b[yi], xr_b[xi]], writes=[xo_b[xi]])
                    stores.append(R.dma("sp", lambda eng, xi=xi, m=m, t0=t0: eng.dma_start(
                        out=dst_v[:, m, t0:t0 + 512], in_=xo[xi][:]), f"xo{xi}", reads=[xo_b[xi]]))
        last = {}
        for o in stores:
            last[id(o.sem)] = o
        R.emit(final_waits=list(last.values()))


def final_norm_phase(nc, tag, src, dst, gain):
    src_v = src.rearrange("(c p) t -> p c t", p=128)
    dst_v = dst.rearrange("(c p) t -> p c t", p=128)
    with ExitStack() as es:
        R = Rec(nc, es, tag)
        N = NormCtx(nc, es, tag)
        g_sb = es.enter_context(nc.sbuf_tensor(f"{tag}_gain", [128, NC], F32))
        g_b = Buf("gain")
        ho = [es.enter_context(nc.sbuf_tensor(f"{tag}_ho{i}", [128, NC, 512], F32)) for i in range(2)]
        ho_b = [Buf(f"ho{i}") for i in range(2)]
        norm_init(R, N)
        R.dma("sp", lambda eng: eng.dma_start(out=g_sb[:], in_=gain), "gain", writes=[g_b])
        stores = []
        for it in range(S // 512):
            i = it % 2
            norm_subtile(R, N, std_loader(src_v, it * 512), g_sb, ho[i], ho_b[i], g_b)
            stores.append(R.dma("sp", lambda eng, i=i, it=it: eng.dma_start(
                out=dst_v[:, :, it * 512:(it + 1) * 512], in_=ho[i][:]), f"ho{i}", reads=[ho_b[i]]))
        last = {}
        for o in stores:
            last[id(o.sem)] = o
        R.emit(final_waits=list(last.values()))


POOL_W = (2, 4, 8, 16)


def pool_phase(nc, tag, src, dst, wp, bp, sp_, gain, pinv):
    H = 8
    W = 512 + 2 * H
    src_v = src.rearrange("(c p) t -> p c t", p=128)
    dst_v = dst.rearrange("(c p) t -> p c t", p=128)
    with ExitStack() as es:
        R = Rec(nc, es, tag)
        sb = lambda name, shape, dt: es.enter_context(nc.sbuf_tensor(f"{tag}_{name}", shape, dt))
        N = NormCtx(nc, es, tag, W=W)
        g_sb = sb("gain", [128, NC], F32); g_b = Buf("gain")
        b_sb = sb("pb", [128, NC], F32); b_b = Buf("pb")
        s_sb = sb("psc", [128, NC], F32); s_b = Buf("psc")
        bs_sb = sb("pbs", [128, NC], F32); bs_b = Buf("pbs")
        pi_sb = sb("pinv", [128, 4, 2, 8], F32); pi_b = Buf("pinv")
        wpb = sb("wp", [128, 4, 2, 256], BF16); wp_b = Buf("wp")
        hn = sb("hn", [128, NC, W], F32); hn_b = Buf("hn")
        NSC = 2
        wsc = [[sb(f"w{k}_{i}", [128, W], F32) for k in range(4)] for i in range(NSC)]
        wsc_b = [[Buf(f"w{k}_{i}") for k in range(4)] for i in range(NSC)]
        t8 = [sb(f"t8_{i}", [128, 8], F32) for i in range(NSC)]
        t8_b = [Buf(f"t8_{i}") for i in range(NSC)]
        pl = sb("pl", [128, NC, 512], BF16)
        pl_b = [Buf(f"pl{c}") for c in range(NC)]
        t1 = [sb(f"t1_{i}", [128, 512], F32) for i in range(2)]
        t1_b = [Buf(f"t1_{i}") for i in range(2)]
        ot = [sb(f"ot{i}", [128, NC, 512], F32) for i in range(2)]
        ot_b = [Buf(f"ot{i}") for i in range(2)]
        yps = [es.enter_context(nc.psum_tensor(f"{tag}_yps{i}", [128, 512], F32)) for i in range(2)]
        yps_b = [Buf(f"yps{i}") for i in range(2)]

        norm_init(R, N)
        R.dma("sp", lambda eng: eng.dma_start(out=g_sb[:], in_=gain), "c0", writes=[g_b])
        R.dma("sp", lambda eng: eng.dma_start(out=b_sb[:], in_=bp), "c1", writes=[b_b])
        R.dma("sp", lambda eng: eng.dma_start(out=s_sb[:], in_=sp_), "c2", writes=[s_b])
        R.dma("sp", lambda eng: eng.dma_start(out=pi_sb[:], in_=pinv), "c3", writes=[pi_b])
        R.dma("pool", lambda eng: eng.dma_start(out=wpb[:].rearrange("p g k n -> p (g k n)"),
                                                in_=wp.rearrange("p g k n -> p (g k n)")), "c4", writes=[wp_b])
        R.op("dve", lambda eng: eng.tensor_tensor(out=bs_sb[:], in0=b_sb[:], in1=s_sb[:], op=ALU.mult),
             reads=[b_b, s_b], writes=[bs_b])
        stores = []
        ksc = 0
        ky = 0
        NT = S // 512
        for it in range(NT):
            t0 = it * 512

            def loader(R, xs, xb, i, it=it, t0=t0):
                if it == 0:
                    R.op("dve", lambda eng: eng.memset(xs[:, :, 0:H], 0.0), writes=[xb])
                    R.dma("sp", lambda eng: eng.dma_start(out=xs[:, :, H:W], in_=src_v[:, :, 0:512 + H]),
                          f"xs{i}", writes=[xb])
                elif it == NT - 1:
                    R.op("dve", lambda eng: eng.memset(xs[:, :, W - H:W], 0.0), writes=[xb])
                    R.dma("sp", lambda eng: eng.dma_start(out=xs[:, :, 0:W - H], in_=src_v[:, :, t0 - H:S]),
                          f"xs{i}", writes=[xb])
                else:
                    R.dma("sp", lambda eng: eng.dma_start(out=xs[:], in_=src_v[:, :, t0 - H:t0 + 512 + H]),
                          f"xs{i}", writes=[xb])

            xs, xb = norm_subtile(R, N, loader, g_sb, hn, hn_b, g_b)
            for c in range(NC):
                g = c // 2
                w = POOL_W[g]
                si = ksc % NSC
                ksc += 1
                ws, wb = wsc[si], wsc_b[si]
                R.op("dve", lambda eng, ws=ws, c=c: eng.tensor_tensor(
                    out=ws[0][:, 1:W], in0=hn[:, c, 0:W - 1], in1=hn[:, c, 1:W], op=ALU.add),
                    reads=[hn_b], writes=[wb[0]])
                if g >= 1:
                    R.op("dve", lambda eng, ws=ws: eng.tensor_tensor(
                        out=ws[1][:, 2:W - 1], in0=ws[0][:, 1:W - 2], in1=ws[0][:, 3:W], op=ALU.add),
                        reads=[wb[0]], writes=[wb[1]])
                if g >= 2:
                    R.op("dve", lambda eng, ws=ws: eng.tensor_tensor(
                        out=ws[2][:, 4:W - 3], in0=ws[1][:, 2:W - 5], in1=ws[1][:, 6:W - 1], op=ALU.add),
                        reads=[wb[1]], writes=[wb[2]])
                if g >= 3:
                    R.op("dve", lambda eng, ws=ws: eng.tensor_tensor(
                        out=ws[3][:, 8:W - 8], in0=ws[2][:, 4:W - 12], in1=ws[2][:, 12:W - 4], op=ALU.add),
                        reads=[wb[2]], writes=[wb[3]])
                wsum, wsum_b = ws[g], wb[g]
                R.op("dve", lambda eng, wsum=wsum, c=c, w=w: eng.scalar_tensor_tensor(
                    out=pl[:, c, :], in0=wsum[:, H:H + 512], scalar=1.0 / w, in1=hn[:, c, H:H + 512],
                    op0=ALU.mult, op1=ALU.subtract),
                    reads=[wsum_b, hn_b], writes=[pl_b[c]])
                for side, (do, so) in ((0, (0, H)), (1, (504, H + 504))):
                    if (side == 0 and it == 0) or (side == 1 and it == NT - 1):
                        R.op("dve", lambda eng, si=si, wsum=wsum, g=g, side=side, so=so: eng.tensor_tensor(
                            out=t8[si][:], in0=wsum[:, so:so + 8], in1=pi_sb[:, g, side, :], op=ALU.mult),
                            reads=[wsum_b, pi_b], writes=[t8_b[si]])
                        R.op("dve", lambda eng, si=si, c=c, do=do, so=so: eng.tensor_tensor(
                            out=pl[:, c, do:do + 8], in0=t8[si][:], in1=hn[:, c, so:so + 8], op=ALU.subtract),
                            reads=[t8_b[si], hn_b], writes=[pl_b[c]])
            oi = it % 2
            for co in range(NC):
                g, mo = co // 2, co % 2
                yi = ky % 2
                ky += 1
                mm_group(R, yps[yi][:], [(wpb[:, g, kc, mo * 128:(mo + 1) * 128], pl[:, 2 * g + kc, :])
                                         for kc in range(2)],
                         reads=[wp_b, pl_b[2 * g], pl_b[2 * g + 1]], bank=yps_b[yi])
                R.op("act", lambda eng, yi=yi, co=co, xs=xs: eng.activation(
                    out=t1[yi][:], in_=xs[:, co, H:H + 512], func=AF.Identity, bias=bs_sb[:, co:co + 1]),
                    reads=[xb, bs_b], writes=[t1_b[yi]])
                R.op("dve", lambda eng, yi=yi, co=co, oi=oi: eng.scalar_tensor_tensor(
                    out=ot[oi][:, co, :], in0=yps[yi][:], scalar=s_sb[:, co:co + 1], in1=t1[yi][:],
                    op0=ALU.mult, op1=ALU.add),
                    reads=[yps_b[yi], t1_b[yi], s_b], writes=[ot_b[oi]])
            stores.append(R.dma("sp", lambda eng, oi=oi, t0=t0: eng.dma_start(
                out=dst_v[:, :, t0:t0 + 512], in_=ot[oi][:]), f"ot{oi}", reads=[ot_b[oi]]))
        last = {}
        for o in stores:
            last[id(o.sem)] = o
        R.emit(final_waits=list(last.values()))


def _evac(R, k, out_ap, in_ap, reads, writes):
    if k % 2 == 0:
        R.op("act", lambda eng: eng.activation(out=out_ap, in_=in_ap, func=AF.Copy), reads=reads, writes=writes)
    else:
        R.op("dve", lambda eng: eng.tensor_copy(out=out_ap, in_=in_ap), reads=reads, writes=writes)


def fourier_phase(nc, tag, src, dst, wf, bf, gain, ccsc, cstab, fscr):
    src_v = src.rearrange("(c p) t -> p c t", p=128)
    dst_v = dst.rearrange("(c p) t -> p c t", p=128)
    f_v = fscr.rearrange("(c p) t -> p c t", p=128)
    NSC = S // 128
    with ExitStack() as es0:
        AB = es0.enter_context(nc.sbuf_tensor(f"{tag}_AB", [128, NSC, 4, 512], BF16))
        with ExitStack() as es:
            t1 = tag + "a"
            R = Rec(nc, es, t1)
            sb = lambda name, shape, dt: es.enter_context(nc.sbuf_tensor(f"{t1}_{name}", shape, dt))
            N = NormCtx(nc, es, t1, nslots=1)
            g_sb = sb("gain", [128, NC], F32); g_b = Buf("gain")
            cc = sb("cc", [128, 2, 512], BF16); cc_b = Buf("cc")
            hT = [sb(f"hT{i}", [128, NC, 512], BF16) for i in range(2)]
            hT_b = [Buf(f"hT{i}") for i in range(2)]
            abp = [es.enter_context(nc.psum_tensor(f"{t1}_abp{i}", [128, 512], F32)) for i in range(4)]
            abp_b = [Buf(f"abp{i}") for i in range(4)]
            AB_b = Buf("AB")
            norm_init(R, N)
            R.dma("sp", lambda eng: eng.dma_start(out=g_sb[:], in_=gain), "c0", writes=[g_b])
            R.dma("pool", lambda eng: eng.dma_start(out=cc[:].rearrange("p k n -> p (k n)"),
                                                    in_=ccsc.rearrange("p k n -> p (k n)")), "c1", writes=[cc_b])
            kp = 0
            last_ev = []
            for it in range(S // 512):
                hi = it % 2
                norm_subtile(R, N, std_loader(src_v, it * 512), g_sb, hT[hi], hT_b[hi], g_b)
                for scl in range(4):
                    si = it * 4 + scl
                    for g in range(4):
                        pi_ = kp % 4
                        kp += 1
                        mm_group(R, abp[pi_][:], [(hT[hi][:, 2 * g + kc, scl * 128:(scl + 1) * 128], cc[:, kc, :])
                                                  for kc in range(2)],
                                 reads=[hT_b[hi], cc_b], bank=abp_b[pi_])
                        _evac(R, kp, AB[:, si, g, :], abp[pi_][:], [abp_b[pi_]], [AB_b])
            R.emit()
        with ExitStack() as es:
            t2 = tag + "b"
            R = Rec(nc, es, t2)
            sb = lambda name, shape, dt: es.enter_context(nc.sbuf_tensor(f"{t2}_{name}", shape, dt))
            NTB = 3
            tb = [sb(f"tb{i}", [128, 2, 4, 512], BF16) for i in range(NTB)]
            tb_b = [Buf(f"tb{i}") for i in range(NTB)]
            fo = [sb(f"fo{i}", [128, NC, 512], BF16) for i in range(2)]
            fo_b = [Buf(f"fo{i}") for i in range(2)]
            acc = [es.enter_context(nc.psum_tensor(f"{t2}_acc{i}", [128, 512], F32)) for i in range(8)]
            acc_b = [Buf(f"acc{i}") for i in range(8)]
            stores = []
            kt_ = 0
            for kt in range(8):
                for blk in range(8):
                    bi = kt_ % NTB
                    kt_ += 1
                    R.dma("sp", lambda eng, bi=bi, kt=kt, blk=blk: eng.dma_start(
                        out=tb[bi][:].rearrange("p a s k -> p (a s k)"),
                        in_=cstab[kt, blk].rearrange("p a s k -> p (a s k)")), f"tb{bi}", writes=[tb_b[bi]])
                    for sc in range(4):
                        si = blk * 4 + sc
                        for co in range(8):
                            g, half = co // 2, co % 2
                            R.op("pe", lambda eng, co=co, si=si, g=g, half=half, bi=bi, sc=sc: eng.matmul(
                                acc[co][:], AB[:, si, g, half * 128:(half + 1) * 128], tb[bi][:, 0, sc, :],
                                start=(si == 0), stop=False),
                                reads=[tb_b[bi]], writes=[acc_b[co]])
                            R.op("pe", lambda eng, co=co, si=si, g=g, half=half, bi=bi, sc=sc: eng.matmul(
                                acc[co][:], AB[:, si, g, 256 + half * 128:256 + (half + 1) * 128], tb[bi][:, 1, sc, :],
                                start=False, stop=(si == NSC - 1)),
                                reads=[tb_b[bi]], writes=[acc_b[co]])
                fi = kt % 2
                for co in range(8):
                    _evac(R, co, fo[fi][:, co, :], acc[co][:], [acc_b[co]], [fo_b[fi]])
                stores.append(R.dma("sp", lambda eng, fi=fi, kt=kt: eng.dma_start(
                    out=f_v[:, :, kt * 512:(kt + 1) * 512], in_=fo[fi][:]), f"fo{fi}", reads=[fo_b[fi]]))
            last = {}
            for o in stores:
                last[id(o.sem)] = o
            R.emit(final_waits=list(last.values()))
    with ExitStack() as es:
        t3 = tag + "c"
        R = Rec(nc, es, t3)
        sb = lambda name, shape, dt: es.enter_context(nc.sbuf_tensor(f"{t3}_{name}", shape, dt))
        wfb = sb("wf", [128, NC, 1024], BF16); wf_b = Buf("wf")
        b_sb = sb("bf", [128, NC], F32); b_b = Buf("bf")
        ft = [sb(f"ft{i}", [128, NC, 512], BF16) for i in range(2)]
        ft_b = [Buf(f"ft{i}") for i in range(2)]
        xt = [sb(f"xt{i}", [128, NC, 512], F32) for i in range(2)]
        xt_b = [Buf(f"xt{i}") for i in range(2)]
        ot = [sb(f"ot{i}", [128, NC, 512], F32) for i in range(2)]
        ot_b = [Buf(f"ot{i}") for i in range(2)]
        zps = [es.enter_context(nc.psum_tensor(f"{t3}_zps{i}", [128, 512], F32)) for i in range(2)]
        zps_b = [Buf(f"zps{i}") for i in range(2)]
        R.dma("sp", lambda eng: eng.dma_start(out=b_sb[:], in_=bf), "c0", writes=[b_b])
        for c in range(NC):
            R.dma("pool", lambda eng, c=c: eng.dma_start(out=wfb[:, c, :], in_=wf[:, c, :]), "c1", writes=[wf_b])
        stores = []
        kz = 0
        for it in range(S // 512):
            i = it % 2
            tsl = slice(it * 512, (it + 1) * 512)
            R.dma("sp", lambda eng, i=i, tsl=tsl: eng.dma_start(out=ft[i][:], in_=f_v[:, :, tsl]), f"ft{i}", writes=[ft_b[i]])
            R.dma("sp", lambda eng, i=i, tsl=tsl: eng.dma_start(out=xt[i][:], in_=src_v[:, :, tsl]), f"xt{i}", writes=[xt_b[i]])
            for m in range(NC):
                zi = kz % 2
                kz += 1
                mm_group(R, zps[zi][:], [(wfb[:, c, m * 128:(m + 1) * 128], ft[i][:, c, :]) for c in range(NC)],
                         reads=[wf_b, ft_b[i]], bank=zps_b[zi])
                R.op("dve", lambda eng, zi=zi, m=m, i=i: eng.scalar_tensor_tensor(
                    out=ot[i][:, m, :], in0=zps[zi][:], scalar=b_sb[:, m:m + 1], in1=xt[i][:, m, :],
                    op0=ALU.add, op1=ALU.add),
                    reads=[zps_b[zi], b_b, xt_b[i]], writes=[ot_b[i]])
            stores.append(R.dma("sp", lambda eng, i=i, tsl=tsl: eng.dma_start(
                out=dst_v[:, :, tsl], in_=ot[i][:]), f"ot{i}", reads=[ot_b[i]]))
        last = {}
        for o in stores:
            last[id(o.sem)] = o
        R.emit(final_waits=list(last.values()))


HD = 128
NQH = 8
NKVH = 2


def attn_phase(nc, tag, src, dst, wqkv, wo, qkg, gain, rotm, cstab):
    src_v = src.rearrange("(c p) t -> p c t", p=128)
    dst_v = dst.rearrange("(c p) t -> p c t", p=128)
    with ExitStack() as es:
        R = Rec(nc, es, tag)
        sb = lambda name, shape, dt: es.enter_context(nc.sbuf_tensor(f"{tag}_{name}", shape, dt))
        ps = lambda name: es.enter_context(nc.psum_tensor(f"{tag}_{name}", [128, 512], F32))
        SS = ps("ss"); SS_b = Buf("ss")
        N = NormCtx(nc, es, tag, nslots=1, ss=SS, ss_b=SS_b)
        A = [ps(f"A{i}") for i in range(2)]; A_b = [Buf(f"A{i}") for i in range(2)]
        ROT = ps("rot"); ROT_b = Buf("rot")
        SC = [ps(f"sc{i}") for i in range(2)]; SC_b = [Buf(f"sc{i}") for i in range(2)]
        OACC = ps("oacc"); OACC_b = Buf("oacc")
        DEN = ps("den"); DEN_b = Buf("den")
        g_sb = sb("gain", [128, NC], F32); g_b = Buf("gain")
        qkg_sb = sb("qkg", [128, 2], F32); qkg_b = Buf("qkg")
        wq = sb("wq", [128, NC, 1024], BF16); wq_b = Buf("wq")
        wkv = sb("wkv", [128, NC, 512], BF16); wkv_b = Buf("wkv")
        wob = sb("wo", [128, NQH, 1024], BF16); wo_b = Buf("wo")
        rot_sb = sb("rotm", [128, 128], BF16); rotm_b = Buf("rotm")
        KT = sb("KT", [128, NKVH, S], BF16)
        KT_b = [Buf(f"KT{i}") for i in range(S // 512)]
        V = sb("V", [128, S // 128, 256], BF16)
        V_b = [Buf(f"V{i}") for i in range(S // 512)]
        hT = sb("hT", [128, NC, 512], BF16); hT_b = Buf("hT")
        cs = sb("cs", [128, 2, 512], F32); cs_b = Buf("cs")
        sqh = [sb(f"sqh{i}", [128, 512], BF16) for i in range(2)]; sqh_b = [Buf(f"sqh{i}") for i in range(2)]
        qg = [sb(f"qg{i}", [128, 512], BF16) for i in range(2)]; qg_b = [Buf(f"qg{i}") for i in range(2)]
        rs = [sb(f"rs{i}", [128, 512], F32) for i in range(2)]; rs_b = [Buf(f"rs{i}") for i in range(2)]
        ta = [sb(f"ta{i}", [128, 512], F32) for i in range(2)]; ta_b = [Buf(f"ta{i}") for i in range(2)]
        tb_ = [sb(f"tb{i}", [128, 512], F32) for i in range(2)]; tb_b = [Buf(f"tb{i}") for i in range(2)]
        QT = sb("QT", [128, NQH, 512], BF16); QT_b = [Buf(f"QT{h}") for h in range(NQH)]
        NPT = 4
        PT = [sb(f"PT{i}", [128, 512], BF16) for i in range(NPT)]; PT_b = [Buf(f"PT{i}") for i in range(NPT)]
        oc = [sb(f"oc{i}", [128, 512], F32) for i in range(2)]; oc_b = [Buf(f"oc{i}") for i in range(2)]
        dn = [sb(f"dn{i}", [128, 512], F32) for i in range(2)]; dn_b = [Buf(f"dn{i}") for i in range(2)]
        OT = sb("OT", [128, NQH, 512], BF16); OT_b = [Buf(f"OT{h}") for h in range(NQH)]
        ot = sb("ot", [128, NC, 512], F32); ot_b = Buf("ot")

        norm_init(R, N)
        R.dma("sp", lambda eng: eng.dma_start(out=g_sb[:], in_=gain), "c0", writes=[g_b])
        R.dma("sp", lambda eng: eng.dma_start(out=qkg_sb[:], in_=qkg), "c1", writes=[qkg_b])
        R.dma("pool", lambda eng: eng.dma_start(out=rot_sb[:], in_=rotm), "c2", writes=[rotm_b])
        for c in range(NC):
            R.dma("pool", lambda eng, c=c: eng.dma_start(out=wkv[:, c, :], in_=wqkv[:, c, 1024:1536]), "c3", writes=[wkv_b])
        for c in range(NC):
            R.dma("pool", lambda eng, c=c: eng.dma_start(out=wq[:, c, :], in_=wqkv[:, c, 0:1024]), "c4", writes=[wq_b])
        for c in range(NC):
            R.dma("pool", lambda eng, c=c: eng.dma_start(out=wob[:, c, :], in_=wo[:, c, :]), "c5", writes=[wo_b])

        cnt = {"a": 0, "n": 0}

        def qk_norm_rope(src_ps, src_b, gcol, is_q, out_ap, out_b):
            i = cnt["n"] % 2
            cnt["n"] += 1
            R.op("act", lambda eng: eng.activation(out=sqh[i][:], in_=src_ps, func=AF.Square),
                 reads=[src_b], writes=[sqh_b[i]])
            R.op("act", lambda eng: eng.activation(out=qg[i][:], in_=src_ps, func=AF.Identity,
                                                   scale=qkg_sb[:, gcol:gcol + 1]),
                 reads=[src_b, qkg_b], writes=[qg_b[i]])
            R.op("pe", lambda eng: eng.matmul(SS[:], N.ones[:], sqh[i][:], start=True, stop=True),
                 reads=[sqh_b[i], N.ones_b], writes=[SS_b])
            R.op("pe", lambda eng: eng.matmul(ROT[:], rot_sb[:], qg[i][:], start=True, stop=True),
                 reads=[qg_b[i], rotm_b], writes=[ROT_b])
            a, b = (1.0, HD * EPS) if is_q else (1.0 / HD, EPS)
            R.op("dve", lambda eng: eng.tensor_scalar(out=rs[i][:], in0=SS[:], scalar1=a, scalar2=b,
                                                      op0=ALU.mult, op1=ALU.add),
                 reads=[SS_b], writes=[rs_b[i]])
            R.op("act", lambda eng: eng.activation(out=rs[i][:], in_=rs[i][:], func=AF.Sqrt),
                 reads=[rs_b[i]], writes=[rs_b[i]])
            R.op("dve", lambda eng: eng.reciprocal(out=rs[i][:], in_=rs[i][:]), reads=[rs_b[i]], writes=[rs_b[i]])
            R.op("pool", lambda eng: eng.tensor_tensor(out=ta[i][:], in0=qg[i][:], in1=cs[:, 0, :], op=ALU.mult),
                 reads=[qg_b[i], cs_b], writes=[ta_b[i]])
            R.op("dve", lambda eng: eng.tensor_tensor(out=tb_[i][:], in0=ROT[:], in1=cs[:, 1, :], op=ALU.mult),
                 reads=[ROT_b, cs_b], writes=[tb_b[i]])
            R.op("pool", lambda eng: eng.tensor_tensor(out=ta[i][:], in0=ta[i][:], in1=tb_[i][:], op=ALU.add),
                 reads=[ta_b[i], tb_b[i]], writes=[ta_b[i]])
            R.op("dve", lambda eng: eng.tensor_tensor(out=out_ap, in0=ta[i][:], in1=rs[i][:], op=ALU.mult),
                 reads=[ta_b[i], rs_b[i]], writes=[out_b])

        for it in range(S // 512):
            tsl = slice(it * 512, (it + 1) * 512)
            norm_subtile(R, N, std_loader(src_v, it * 512), g_sb, hT, hT_b, g_b)
            R.dma("sp", lambda eng, tsl=tsl: eng.dma_start(out=cs[:], in_=cstab[:, :, tsl]), "cs", writes=[cs_b])
            for kvh in range(NKVH):
                ai = cnt["a"] % 2
                cnt["a"] += 1
                mm_group(R, A[ai][:], [(wkv[:, c, kvh * 128:(kvh + 1) * 128], hT[:, c, :]) for c in range(NC)],
                         reads=[wkv_b, hT_b], bank=A_b[ai])
                qk_norm_rope(A[ai][:], A_b[ai], 1, False, KT[:, kvh, tsl], KT_b[it])
            for scl in range(4):
                ai = cnt["a"] % 2
                cnt["a"] += 1
                mm_group(R, A[ai][:, 0:256], [(hT[:, c, scl * 128:(scl + 1) * 128], wkv[:, c, 256:512])
                                              for c in range(NC)],
                         reads=[wkv_b, hT_b], bank=A_b[ai])
                _evac(R, scl, V[:, it * 4 + scl, :], A[ai][:, 0:256], [A_b[ai]], [V_b[it]])

        stores = []
        kpt = 0
        ksc = 0
        ko = 0
        for qt in range(S // 512):
            tsl = slice(qt * 512, (qt + 1) * 512)
            xs, xb = norm_subtile(R, N, std_loader(src_v, qt * 512), g_sb, hT, hT_b, g_b)
            R.dma("sp", lambda eng, tsl=tsl: eng.dma_start(out=cs[:], in_=cstab[:, :, tsl]), "cs", writes=[cs_b])
            for h in range(NQH):
                ai = cnt["a"] % 2
                cnt["a"] += 1
                mm_group(R, A[ai][:], [(wq[:, c, h * 128:(h + 1) * 128], hT[:, c, :]) for c in range(NC)],
                         reads=[wq_b, hT_b], bank=A_b[ai])
                qk_norm_rope(A[ai][:], A_b[ai], 0, True, QT[:, h, :], QT_b[h])
            allK = KT_b
            allV = V_b
            for h in range(NQH):
                kvh = h // 4
                NKC = S // 128
                pend = []

                def qk(kc, h=h, kvh=kvh):
                    nonlocal ksc
                    si = ksc % 2
                    ksc += 1
                    R.op("pe", lambda eng, si=si, kc=kc: eng.matmul(
                        SC[si][:], KT[:, kvh, kc * 128:(kc + 1) * 128], QT[:, h, :], start=True, stop=True),
                        reads=[QT_b[h]] + (allK if kc == 0 else []), writes=[SC_b[si]])
                    return si

                def ex(si):
                    nonlocal kpt
                    pi_ = kpt % NPT
                    kpt += 1
                    R.op("act", lambda eng, si=si, pi_=pi_: eng.activation(out=PT[pi_][:], in_=SC[si][:], func=AF.Exp),
                         reads=[SC_b[si]], writes=[PT_b[pi_]])
                    return pi_

                def pv(kc, pi_, kvh=kvh):
                    R.op("pe", lambda eng, kc=kc, pi_=pi_: eng.matmul(
                        OACC[:], V[:, kc, kvh * 128:(kvh + 1) * 128], PT[pi_][:],
                        start=(kc == 0), stop=(kc == NKC - 1)),
                        reads=[PT_b[pi_]] + (allV if kc == 0 else []), writes=[OACC_b])
                    R.op("pe", lambda eng, kc=kc, pi_=pi_: eng.matmul(
                        DEN[:], N.ones[:], PT[pi_][:], start=(kc == 0), stop=(kc == NKC - 1)),
                        reads=[PT_b[pi_]], writes=[DEN_b])

                s0 = qk(0)
                p_prev = ex(s0)
                for kc in range(1, NKC):
                    s1 = qk(kc)
                    pv(kc - 1, p_prev)
                    p_prev = ex(s1)
                pv(NKC - 1, p_prev)
                oi = ko % 2
                ko += 1
                R.op("act", lambda eng, oi=oi: eng.activation(out=oc[oi][:], in_=OACC[:], func=AF.Copy),
                     reads=[OACC_b], writes=[oc_b[oi]])
                R.op("dve", lambda eng, oi=oi: eng.tensor_copy(out=dn[oi][:], in_=DEN[:]),
                     reads=[DEN_b], writes=[dn_b[oi]])
                R.op("dve", lambda eng, oi=oi: eng.reciprocal(out=dn[oi][:], in_=dn[oi][:]),
                     reads=[dn_b[oi]], writes=[dn_b[oi]])
                R.op("dve", lambda eng, oi=oi, h=h: eng.tensor_tensor(out=OT[:, h, :], in0=oc[oi][:], in1=dn[oi][:],
                                                                     op=ALU.mult),
                     reads=[oc_b[oi], dn_b[oi]], writes=[OT_b[h]])
            for m in range(NC):
                ai = cnt["a"] % 2
                cnt["a"] += 1
                mm_group(R, A[ai][:], [(wob[:, h, m * 128:(m + 1) * 128], OT[:, h, :]) for h in range(NQH)],
                         reads=[wo_b] + OT_b, bank=A_b[ai])
                R.op("dve", lambda eng, ai=ai, m=m, xs=xs: eng.tensor_tensor(
                    out=ot[:, m, :], in0=A[ai][:], in1=xs[:, m, :], op=ALU.add),
                    reads=[A_b[ai], xb], writes=[ot_b])
            stores.append(R.dma("sp", lambda eng, tsl=tsl: eng.dma_start(out=dst_v[:, :, tsl], in_=ot[:]),
                                "ot", reads=[ot_b]))
        R.emit(final_waits=[stores[-1]])


def build_program(phases):
    nc = bass.Bass("TRN2", target_bir_lowering=False)
    ext = lambda name, shape: nc.dram_tensor(name, shape, F32, kind="ExternalInput").ap()
    xT = ext("xT", [D, S])
    yT = nc.dram_tensor("yT", [D, S], F32, kind="ExternalOutput").ap()
    wts = {}
    for f in ("ffn1", "ffn2"):
        wts[f + "_wg"] = ext(f + "_wg", [DEPTH, NJ, 128, NC, 128])
        wts[f + "_wu"] = ext(f + "_wu", [DEPTH, NJ, 128, NC, 128])
        wts[f + "_wd"] = ext(f + "_wd", [DEPTH, NC, 128, NJ, 128])
        wts[f + "_g"] = ext(f + "_g", [DEPTH, 128, NC])
    wts["mix_g"] = ext("mix_g", [DEPTH, 128, NC])
    wts["fin_g"] = ext("fin_g", [128, NC])
    wts["pool_w"] = ext("pool_w", [2, 128, 4, 2, 256])
    wts["pool_b"] = ext("pool_b", [2, 128, NC])
    wts["pool_s"] = ext("pool_s", [2, 128, NC])
    wts["pool_inv"] = ext("pool_inv", [128, 4, 2, 8])
    wts["f_w"] = ext("f_w", [128, NC, 1024])
    wts["f_b"] = ext("f_b", [128, NC])
    wts["f_ccsc"] = ext("f_ccsc", [128, 2, 512])
    wts["f_cstab"] = nc.dram_tensor("f_cstab", [8, 8, 128, 2, 4, 512], BF16, kind="ExternalInput").ap()
    fscr = nc.dram_tensor("f_scr", [D, S], BF16).ap()
    wts["a_wqkv"] = ext("a_wqkv", [128, NC, 1536])
    wts["a_wo"] = ext("a_wo", [128, NQH, 1024])
    wts["a_qkg"] = ext("a_qkg", [128, 2])
    wts["a_rotm"] = ext("a_rotm", [128, 128])
    wts["a_cs"] = ext("a_cs", [128, 2, S])
    ra = nc.dram_tensor("res_a", [D, S], F32).ap()
    rb = nc.dram_tensor("res_b", [D, S], F32).ap()
    bufs = [ra, rb]
    cur = xT
    k = 0
    for pi, ph in enumerate(phases):
        last = pi == len(phases) - 1
        dst = yT if last else bufs[k % 2]
        kind = ph[0]
        tag = f"p{pi}"
        if kind == "ffn":
            _, f, layer = ph
            ffn_phase(nc, tag, cur, dst, wts[f + "_wg"][layer], wts[f + "_wu"][layer],
                      wts[f + "_wd"][layer], wts[f + "_g"][layer])
        elif kind == "pool":
            _, layer, j = ph
            pool_phase(nc, tag, cur, dst, wts["pool_w"][j], wts["pool_b"][j], wts["pool_s"][j],
                       wts["mix_g"][layer], wts["pool_inv"])
        elif kind == "fourier":
            _, layer, j = ph
            fourier_phase(nc, tag, cur, dst, wts["f_w"], wts["f_b"], wts["mix_g"][layer],
                          wts["f_ccsc"], wts["f_cstab"], fscr)
        elif kind == "attn":
            _, layer, j = ph
            attn_phase(nc, tag, cur, dst, wts["a_wqkv"], wts["a_wo"], wts["a_qkg"], wts["mix_g"][layer],
                       wts["a_rotm"], wts["a_cs"])
        elif kind == "final":
            final_norm_phase(nc, tag, cur, dst, wts["fin_g"])
        else:
            raise ValueError(kind)
        cur = dst
        k += 1
    return nc


def _pc(v):
    v = np.asarray(v, np.float32)
    return np.ascontiguousarray(np.swapaxes(v.reshape(v.shape[:-1] + (NC, 128)), -1, -2))


def const_tables():
    t = {}
    pinv = np.zeros((4, 2, 8), np.float32)
    for g, w in enumerate(POOL_W):
        for side in range(2):
            for i in range(8):
                tt = i if side == 0 else S - 8 + i
                lo = max(tt - w // 2, 0)
                hi = min(tt + w // 2, S)
                pinv[g, side, i] = 1.0 / float(hi - lo)
    t["pool_inv"] = np.ascontiguousarray(np.broadcast_to(pinv[None], (128, 4, 2, 8)))
    rows = S // 64
    row = np.repeat(np.arange(rows, dtype=np.float32), 64)
    col = np.tile(np.arange(64, dtype=np.float32), rows)
    inv_freq = (np.float32(10000.0) ** (-np.arange(0, 64, 2, dtype=np.float32) / np.float32(64))).astype(np.float32)
    ang_r = row[:, None] * inv_freq[None, :]
    ang_c = col[:, None] * inv_freq[None, :]
    ang = np.concatenate([ang_r, ang_r, ang_c, ang_c], axis=-1).astype(np.float32)
    t["a_cs"] = np.ascontiguousarray(np.stack([np.cos(ang).T, np.sin(ang).T], axis=1)).astype(np.float32)
    rm = np.zeros((128, 128), np.float32)
    for d in range(128):
        if (d % 64) < 32:
            rm[d + 32, d] = -1.0
        else:
            rm[d - 32, d] = 1.0
    t["a_rotm"] = rm
    ch = np.arange(256, dtype=np.int64)
    ang = 2.0 * np.pi * ((ch[:, None] * ch[None, :]) % 256).astype(np.float64) / 256.0
    cc = np.concatenate([np.cos(ang), -np.sin(ang)], axis=1) / 16.0
    t["f_ccsc"] = np.ascontiguousarray(cc.reshape(2, 128, 512).transpose(1, 0, 2)).astype(np.float32)
    n = np.arange(S, dtype=np.float64)
    ct = (np.cos(2.0 * np.pi * n / S) / 64.0).astype(np.float32)
    st = (np.sin(2.0 * np.pi * n / S) / 64.0).astype(np.float32)
    s_idx = np.arange(S, dtype=np.int64)
    prod = (s_idx[:, None] * s_idx[None, :]) % S
    tab = np.stack([ct[prod], st[prod]], axis=0).astype(BF16_NP)
    tab = tab.reshape(2, 8, 4, 128, 8, 512)
    t["f_cstab"] = np.ascontiguousarray(tab.transpose(4, 1, 3, 0, 2, 5))
    return t


def prep_inputs(inp):
    x = np.asarray(inp["x"], np.float32)
    com = dict(const_tables())
    for f in ("ffn1", "ffn2"):
        wg = np.asarray(inp[f + "_w_gate"], np.float32)
        wu = np.asarray(inp[f + "_w_up"], np.float32)
        wd = np.asarray(inp[f + "_w_down"], np.float32)
        com[f + "_wg"] = np.ascontiguousarray(wg.reshape(DEPTH, NC, 128, NJ, 128).transpose(0, 3, 2, 1, 4))
        com[f + "_wu"] = np.ascontiguousarray(wu.reshape(DEPTH, NC, 128, NJ, 128).transpose(0, 3, 2, 1, 4))
        com[f + "_wd"] = np.ascontiguousarray(wd.reshape(DEPTH, NJ, 128, NC, 128).transpose(0, 3, 2, 1, 4))
        com[f + "_g"] = _pc(inp[f + "_norm"])
    com["mix_g"] = _pc(inp["mixer_norm"])
    com["fin_g"] = _pc(inp["final_norm"])
    pw = np.asarray(inp["pool_w"], np.float32)
    com["pool_w"] = np.ascontiguousarray(pw.reshape(2, 4, 2, 128, 256).transpose(0, 3, 1, 2, 4))
    com["pool_b"] = _pc(np.asarray(inp["pool_b"], np.float32).reshape(2, D))
    com["pool_s"] = _pc(inp["pool_scale"])
    wqkv = np.asarray(inp["attn_w_qkv"], np.float32)[0]
    com["a_wqkv"] = np.ascontiguousarray(wqkv.reshape(NC, 128, 1536).transpose(1, 0, 2))
    wo = np.asarray(inp["attn_w_o"], np.float32)[0]
    com["a_wo"] = np.ascontiguousarray(wo.reshape(NQH, 128, 1024).transpose(1, 0, 2))
    com["a_qkg"] = np.ascontiguousarray(np.stack([np.asarray(inp["attn_q_norm"], np.float32)[0],
                                                  np.asarray(inp["attn_k_norm"], np.float32)[0]], axis=1))
    fw = np.asarray(inp["fourier_w"], np.float32)[0]
    com["f_w"] = np.ascontiguousarray(fw.reshape(NC, 128, 1024).transpose(1, 0, 2))
    com["f_b"] = _pc(np.asarray(inp["fourier_b"], np.float32)[0])
    maps = []
    for b in range(NB):
        m = dict(com)
        m["xT"] = np.ascontiguousarray(x[b].T)
        maps.append(m)
    return maps


def run_phases(inp, phases):
    nc = build_program(phases)
    maps = prep_inputs(inp)
    res = run_bass_kernel_spmd(nc, maps, core_ids=list(range(NB)))
    out = np.stack([np.ascontiguousarray(r["yT"].T) for r in res.results], axis=0)
    return out


def full_phases():
    phases = []
    for i in range(DEPTH):
        phases.append(("ffn", "ffn1", i))
        kind = i % 3
        if kind == 0:
            phases.append(("pool", i, i // 3))
        elif kind == 1:
            phases.append(("fourier", i, i // 3))
        else:
            phases.append(("attn", i, i // 3))
        phases.append(("ffn", "ffn2", i))
    phases.append(("final",))
    return phases


def kernel(**inputs):
    return run_phases(inputs, full_phases()).astype(np.float32)
```

```python
import numpy as np
import ml_dtypes
from contextlib import ExitStack
import concourse.bass as bass
import concourse.mybir as mybir
from concourse.bass_utils import run_bass_kernel_spmd

F32 = mybir.dt.float32
BF16 = mybir.dt.bfloat16
AF = mybir.ActivationFunctionType
ALU = mybir.AluOpType

D = 1024
S = 4096
NB = 8
DFF = 2816
NJ = DFF // 128
NC = D // 128
DEPTH = 4
EPS = 1e-6
BF16_NP = ml_dtypes.bfloat16


class Buf:
    __slots__ = ("name", "lw", "rd")

    def __init__(self, name):
        self.name = name
        self.lw = None
        self.rd = []


class Op:
    __slots__ = ("eng", "fn", "deps", "signal", "val", "sem", "is_dma")

    def __init__(self, eng, fn):
        self.eng = eng
        self.fn = fn
        self.deps = []
        self.signal = False
        self.val = None
        self.sem = None
        self.is_dma = False


class Rec:
    CE = ("pe", "act", "dve", "pool")

    def __init__(self, nc, es, tag):
        self.nc = nc
        self.es = es
        self.tag = tag
        self.ops = {e: [] for e in ("pe", "act", "dve", "pool", "sp")}
        self.all_sems = []
        self.esem = {e: self._alloc(f"{tag}_s_{e}") for e in self.CE}
        self.dsem = {}

    def _alloc(self, name):
        h = self.nc.alloc_semaphore(name=name)
        self.all_sems.append(h)
        return h

    def _dma_sem(self, name):
        if name not in self.dsem:
            self.dsem[name] = [self._alloc(f"{self.tag}_d_{name}"), 0]
        return self.dsem[name]

    def _hazards(self, o, reads, writes):
        deps = []
        for b in reads:
            if b.lw is not None:
                deps.append(b.lw)
        for b in writes:
            if b.lw is not None:
                deps.append(b.lw)
            deps.extend(b.rd)
        for d in deps:
            if d is o:
                continue
            if d.eng == "pe" and o.eng == "pe":
                continue
            if not d.is_dma:
                d.signal = True
            o.deps.append(d)
        for b in reads:
            b.rd.append(o)
        for b in writes:
            b.lw = o
            b.rd = []

    def op(self, eng, fn, reads=(), writes=()):
        o = Op(eng, fn)
        self._hazards(o, reads, writes)
        self.ops[eng].append(o)
        return o

    def dma(self, queue, fn, sem, reads=(), writes=()):
        o = Op(queue, fn)
        o.is_dma = True
        s = self._dma_sem(sem)
        s[1] += 16
        o.sem = s[0]
        o.val = s[1]
        self._hazards(o, reads, writes)
        self.ops[queue].append(o)
        return o

    def emit(self, final_waits=()):
        nc = self.nc
        for e in self.CE:
            cnt = 0
            for o in self.ops[e]:
                if o.is_dma:
                    continue
                if o.signal:
                    cnt += 1
                    o.val = cnt
                    o.sem = self.esem[e]
        ops = self.ops

        def run(eng, name, extra=()):
            waited = {}
            for o in ops[name]:
                for d in o.deps:
                    key = id(d.sem)
                    if waited.get(key, 0) >= d.val:
                        continue
                    waited[key] = d.val
                    eng.wait_ge(d.sem, d.val)
                ins = o.fn(eng)
                if o.is_dma:
                    ins.then_inc(o.sem, 16)
                elif o.signal:
                    ins.then_inc(o.sem, 1)
            for d in extra:
                eng.wait_ge(d.sem, d.val)

        with nc.Block() as block:
            @block.tensor
            def _(eng):
                run(eng, "pe")

            @block.scalar
            def _(eng):
                run(eng, "act")

            @block.vector
            def _(eng):
                run(eng, "dve")

            @block.gpsimd
            def _(eng):
                run(eng, "pool")

            @block.sync
            def _(eng):
                run(eng, "sp", final_waits)

        nc.all_engine_barrier()
        nc.clear_and_free_semaphores(self.all_sems)
        nc.all_engine_barrier()


def mm_group(R, out_ap, pairs, reads, bank):
    n = len(pairs)
    last = None
    for i, (l, r) in enumerate(pairs):
        def fn(eng, l=l, r=r, i=i):
            return eng.matmul(out_ap, l, r, start=(i == 0), stop=(i == n - 1))
        if i == 0:
            last = R.op("pe", fn, reads=reads, writes=[bank])
        else:
            last = R.op("pe", fn, reads=reads if i == n - 1 else (), writes=[bank])
    return last


class NormCtx:
    def __init__(self, nc, es, tag, nslots=2, W=512, ss=None, ss_b=None):
        self.ns = nslots
        self.W = W
        self.xs = [es.enter_context(nc.sbuf_tensor(f"{tag}_xs{i}", [128, NC, W], F32)) for i in range(nslots)]
        self.xs_b = [Buf(f"xs{i}") for i in range(nslots)]
        self.sq = es.enter_context(nc.sbuf_tensor(f"{tag}_sq", [128, NC, W], BF16))
        self.sq_b = Buf("sq")
        self.ms = [es.enter_context(nc.sbuf_tensor(f"{tag}_ms{i}", [128, W], F32)) for i in range(2)]
        self.ms_b = [Buf(f"ms{i}") for i in range(2)]
        self.ones = es.enter_context(nc.sbuf_tensor(f"{tag}_ones", [128, 128], BF16))
        self.ones_b = Buf("ones")
        if ss is None:
            nb = (W + 511) // 512
            self.ss = es.enter_context(nc.psum_tensor(f"{tag}_ssps", [128, 512 * nb], F32))
            self.ss_b = Buf("ssps")
        else:
            self.ss, self.ss_b = ss, ss_b
        self.k = 0


def norm_init(R, N):
    R.op("dve", lambda eng: eng.memset(N.ones[:], 1.0), writes=[N.ones_b])


def norm_thunks(R, N, loader, gain_ap, h_out, h_buf, gain_buf, info=None):
    i = N.k % N.ns
    mi = N.k % 2
    N.k += 1
    W = N.W
    xs, xb = N.xs[i], N.xs_b[i]
    ms, mb = N.ms[mi], N.ms_b[mi]
    if info is not None:
        info["xs"], info["xb"] = xs, xb
    th = []

    def t_load():
        loader(R, xs, xb, i)
        R.op("act", lambda eng: eng.activation(out=N.sq[:], in_=xs[:], func=AF.Square),
             reads=[xb], writes=[N.sq_b])
    th.append(t_load)

    def t_mm():
        for c0 in range(0, W, 512):
            c1 = min(W, c0 + 512)
            mm_group(R, N.ss[:, c0:c1], [(N.ones[:], N.sq[:, c, c0:c1]) for c in range(NC)],
                     reads=[N.sq_b, N.ones_b], bank=N.ss_b)
    th.append(t_mm)
    th.append(lambda: R.op("dve", lambda eng: eng.tensor_scalar(
        out=ms[:], in0=N.ss[:, 0:W], scalar1=1.0 / D, scalar2=EPS, op0=ALU.mult, op1=ALU.add),
        reads=[N.ss_b], writes=[mb]))
    th.append(lambda: R.op("act", lambda eng: eng.activation(out=ms[:], in_=ms[:], func=AF.Sqrt),
                           reads=[mb], writes=[mb]))
    th.append(lambda: R.op("dve", lambda eng: eng.reciprocal(out=ms[:], in_=ms[:]), reads=[mb], writes=[mb]))
    for c in range(NC):
        th.append(lambda c=c: R.op("dve", lambda eng: eng.scalar_tensor_tensor(
            out=h_out[:, c, :], in0=xs[:, c, :], scalar=gain_ap[:, c:c + 1], in1=ms[:],
            op0=ALU.mult, op1=ALU.mult),
            reads=[xb, mb, gain_buf], writes=[h_buf]))
    return th


def norm_subtile(R, N, loader, gain_ap, h_out, h_buf, gain_buf):
    info = {}
    for t in norm_thunks(R, N, loader, gain_ap, h_out, h_buf, gain_buf, info):
        t()
    return info["xs"], info["xb"]


def std_loader(src_v, t0):
    def ld(R, xs, xb, i):
        R.dma("sp", lambda eng: eng.dma_start(out=xs[:], in_=src_v[:, :, t0:t0 + 512]),
              f"xs{i}", writes=[xb])
    return ld


def ffn_phase(nc, tag, src, dst, wg, wu, wd, gain, T=1024, final_out=False):
    NS = T // 512
    src_v = src.rearrange("(c p) t -> p c t", p=128)
    dst_v = dst.rearrange("(c p) t -> p c t", p=128)
    with ExitStack() as es:
        R = Rec(nc, es, tag)
        sb = lambda name, shape, dt: es.enter_context(nc.sbuf_tensor(f"{tag}_{name}", shape, dt))
        ps = lambda name: es.enter_context(nc.psum_tensor(f"{tag}_{name}", [128, 512], F32))
        N = NormCtx(nc, es, tag)
        g_sb = sb("gain", [128, NC], F32)
        g_b = Buf("gain")
        hTs = [sb(f"hT{i}", [128, NC, T], BF16) for i in range(2)]
        hTs_b = [[Buf(f"hT{i}_{n}") for n in range(NS)] for i in range(2)]
        aT = sb("aT", [128, NJ, T], BF16)
        aT_b = [[Buf(f"aT{j}_{n}") for n in range(NS)] for j in range(NJ)]
        NW = 3
        wgb = [sb(f"wg{i}", [128, NC, 128], BF16) for i in range(NW)]
        wub = [sb(f"wu{i}", [128, NC, 128], BF16) for i in range(NW)]
        wgb_b = [Buf(f"wg{i}") for i in range(NW)]
        wub_b = [Buf(f"wu{i}") for i in range(NW)]
        NWD = 2
        wdb = [sb(f"wd{i}", [128, NJ, 128], BF16) for i in range(NWD)]
        wdb_b = [Buf(f"wd{i}") for i in range(NWD)]
        ssb = [sb(f"s{i}", [128, 512], F32) for i in range(2)]
        ssb_b = [Buf(f"s{i}") for i in range(2)]
        NX = 3
        xr = [sb(f"xr{i}", [128, 512], F32) for i in range(NX)]
        xr_b = [Buf(f"xr{i}") for i in range(NX)]
        xo = [sb(f"xo{i}", [128, 512], F32) for i in range(NX)]
        xo_b = [Buf(f"xo{i}") for i in range(NX)]
        gps = [ps(f"gps{i}") for i in range(2)]
        ups = [ps(f"ups{i}") for i in range(2)]
        yps = [ps(f"yps{i}") for i in range(2)]
        gps_b = [Buf(f"gps{i}") for i in range(2)]
        ups_b = [Buf(f"ups{i}") for i in range(2)]
        yps_b = [Buf(f"yps{i}") for i in range(2)]

        norm_init(R, N)
        R.dma("sp", lambda eng: eng.dma_start(out=g_sb[:], in_=gain), "gain", writes=[g_b])
        stores = []
        kw = 0
        kd = 0
        kb = 0
        ky = 0
        kx = 0
        NST = S // T

        def norm_list(st):
            th = []
            for n in range(NS):
                th += norm_thunks(R, N, std_loader(src_v, st * T + n * 512), g_sb,
                                  hTs[st % 2][:, :, n * 512:(n + 1) * 512], hTs_b[st % 2][n], g_b)
            return th

        for t in norm_list(0):
            t()
        for st in range(NST):
            tb = st * T
            hT, hT_b = hTs[st % 2], hTs_b[st % 2]
            pending = norm_list(st + 1) if st + 1 < NST else []
            step = 0
            for j in range(NJ):
                wi = kw % NW
                kw += 1
                R.dma("pool", lambda eng, wi=wi, j=j: eng.dma_start(
                    out=wgb[wi][:].rearrange("p c m -> p (c m)"),
                    in_=wg[j].rearrange("p c m -> p (c m)")), f"wg{wi}", writes=[wgb_b[wi]])
                R.dma("pool", lambda eng, wi=wi, j=j: eng.dma_start(
                    out=wub[wi][:].rearrange("p c m -> p (c m)"),
                    in_=wu[j].rearrange("p c m -> p (c m)")), f"wu{wi}", writes=[wub_b[wi]])
                for n in range(NS):
                    bi = kb % 2
                    kb += 1
                    tsl = slice(n * 512, (n + 1) * 512)
                    mm_group(R, gps[bi][:], [(wgb[wi][:, c, :], hT[:, c, tsl]) for c in range(NC)],
                             reads=[wgb_b[wi], hT_b[n]], bank=gps_b[bi])
                    mm_group(R, ups[bi][:], [(wub[wi][:, c, :], hT[:, c, tsl]) for c in range(NC)],
                             reads=[wub_b[wi], hT_b[n]], bank=ups_b[bi])
                    R.op("act", lambda eng, bi=bi: eng.activation(out=ssb[bi][:], in_=gps[bi][:], func=AF.Silu),
                         reads=[gps_b[bi]], writes=[ssb_b[bi]])
                    R.op("dve", lambda eng, bi=bi, j=j, tsl=tsl: eng.tensor_tensor(
                        out=aT[:, j, tsl], in0=ssb[bi][:], in1=ups[bi][:], op=ALU.mult),
                        reads=[ssb_b[bi], ups_b[bi]], writes=[aT_b[j][n]])
                    step += 1
                    if step > 6 and pending:
                        pending.pop(0)()
            while pending:
                pending.pop(0)()
            for m in range(NC):
                di = kd % NWD
                kd += 1
                R.dma("pool", lambda eng, di=di, m=m: eng.dma_start(
                    out=wdb[di][:].rearrange("p (a b) q -> p a (b q)", a=2),
                    in_=wd[m].rearrange("p (a b) q -> p a (b q)", a=2)), f"wd{di}", writes=[wdb_b[di]])
                for n in range(NS):
                    yi = ky % 2
                    ky += 1
                    xi = kx % NX
                    kx += 1
                    t0 = tb + n * 512
                    tsl = slice(n * 512, (n + 1) * 512)
                    R.dma("sp", lambda eng, xi=xi, m=m, t0=t0: eng.dma_start(
                        out=xr[xi][:], in_=src_v[:, m, t0:t0 + 512]), f"xr{xi}", writes=[xr_b[xi]])
                    mm_group(R, yps[yi][:], [(wdb[di][:, j, :], aT[:, j, tsl]) for j in range(NJ)],
                             reads=[wdb_b[di]] + [aT_b[j][n] for j in range(NJ)], bank=yps_b[yi])
                    R.op("dve", lambda eng, yi=yi, xi=xi: eng.scalar_tensor_tensor(
                        out=xo[xi][:], in0=yps[yi][:], scalar=0.5, in1=xr[xi][:],
                        op0=ALU.mult, op1=ALU.add),
                        reads=[yps_b[yi], xr_b[xi]], writes=[xo_b[xi]])
                    stores.append(R.dma("sp", lambda eng, xi=xi, m=m, t0=t0: eng.dma_start(
                        out=dst_v[:, m, t0:t0 + 512], in_=xo[xi][:]), f"xo{xi}", reads=[xo_b[xi]]))
        last = {}
        for o in stores:
            last[id(o.sem)] = o
        R.emit(final_waits=list(last.values()))


def final_norm_phase(nc, tag, src, dst, gain):
    src_v = src.rearrange("(c p) t -> p c t", p=128)
    dst_v = dst.rearrange("(c p) t -> p c t", p=128)
    with ExitStack() as es:
        R = Rec(nc, es, tag)
        N = NormCtx(nc, es, tag)
        g_sb = es.enter_context(nc.sbuf_tensor(f"{tag}_gain", [128, NC], F32))
        g_b = Buf("gain")
        ho = [es.enter_context(nc.sbuf_tensor(f"{tag}_ho{i}", [128, NC, 512], F32)) for i in range(2)]
        ho_b = [Buf(f"ho{i}") for i in range(2)]
        norm_init(R, N)
        R.dma("sp", lambda eng: eng.dma_start(out=g_sb[:], in_=gain), "gain", writes=[g_b])
        stores = []
        for it in range(S // 512):
            i = it % 2
            norm_subtile(R, N, std_loader(src_v, it * 512), g_sb, ho[i], ho_b[i], g_b)
            stores.append(R.dma("sp", lambda eng, i=i, it=it: eng.dma_start(
                out=dst_v[:, :, it * 512:(it + 1) * 512], in_=ho[i][:]), f"ho{i}", reads=[ho_b[i]]))
        last = {}
        for o in stores:
            last[id(o.sem)] = o
        R.emit(final_waits=list(last.values()))


POOL_W = (2, 4, 8, 16)


def pool_phase(nc, tag, src, dst, wp, bp, sp_, gain, pinv):
    H = 8
    W = 512 + 2 * H
    src_v = src.rearrange("(c p) t -> p c t", p=128)
    dst_v = dst.rearrange("(c p) t -> p c t", p=128)
    with ExitStack() as es:
        R = Rec(nc, es, tag)
        sb = lambda name, shape, dt: es.enter_context(nc.sbuf_tensor(f"{tag}_{name}", shape, dt))
        N = NormCtx(nc, es, tag, W=W)
        g_sb = sb("gain", [128, NC], F32); g_b = Buf("gain")
        b_sb = sb("pb", [128, NC], F32); b_b = Buf("pb")
        s_sb = sb("psc", [128, NC], F32); s_b = Buf("psc")
        bs_sb = sb("pbs", [128, NC], F32); bs_b = Buf("pbs")
        pi_sb = sb("pinv", [128, 4, 2, 8], F32); pi_b = Buf("pinv")
        wpb = sb("wp", [128, 4, 2, 256], BF16); wp_b = Buf("wp")
        hn = sb("hn", [128, NC, W], F32); hn_b = Buf("hn")
        NSC = 2
        wsc = [[sb(f"w{k}_{i}", [128, W], F32) for k in range(4)] for i in range(NSC)]
        wsc_b = [[Buf(f"w{k}_{i}") for k in range(4)] for i in range(NSC)]
        t8 = [sb(f"t8_{i}", [128, 8], F32) for i in range(NSC)]
        t8_b = [Buf(f"t8_{i}") for i in range(NSC)]
        pl = sb("pl", [128, NC, 512], BF16)
        pl_b = [Buf(f"pl{c}") for c in range(NC)]
        t1 = [sb(f"t1_{i}", [128, 512], F32) for i in range(2)]
        t1_b = [Buf(f"t1_{i}") for i in range(2)]
        ot = [sb(f"ot{i}", [128, NC, 512], F32) for i in range(2)]
        ot_b = [Buf(f"ot{i}") for i in range(2)]
        yps = [es.enter_context(nc.psum_tensor(f"{tag}_yps{i}", [128, 512], F32)) for i in range(2)]
        yps_b = [Buf(f"yps{i}") for i in range(2)]

        norm_init(R, N)
        R.dma("sp", lambda eng: eng.dma_start(out=g_sb[:], in_=gain), "c0", writes=[g_b])
        R.dma("sp", lambda eng: eng.dma_start(out=b_sb[:], in_=bp), "c1", writes=[b_b])
        R.dma("sp", lambda eng: eng.dma_start(out=s_sb[:], in_=sp_), "c2", writes=[s_b])
        R.dma("sp", lambda eng: eng.dma_start(out=pi_sb[:], in_=pinv), "c3", writes=[pi_b])
        R.dma("pool", lambda eng: eng.dma_start(out=wpb[:].rearrange("p g k n -> p (g k n)"),
                                                in_=wp.rearrange("p g k n -> p (g k n)")), "c4", writes=[wp_b])
        R.op("dve", lambda eng: eng.tensor_tensor(out=bs_sb[:], in0=b_sb[:], in1=s_sb[:], op=ALU.mult),
             reads=[b_b, s_b], writes=[bs_b])
        stores = []
        ksc = 0
        ky = 0
        NT = S // 512
        for it in range(NT):
            t0 = it * 512

            def loader(R, xs, xb, i, it=it, t0=t0):
                if it == 0:
                    R.op("dve", lambda eng: eng.memset(xs[:, :, 0:H], 0.0), writes=[xb])
                    R.dma("sp", lambda eng: eng.dma_start(out=xs[:, :, H:W], in_=src_v[:, :, 0:512 + H]),
                          f"xs{i}", writes=[xb])
                elif it == NT - 1:
                    R.op("dve", lambda eng: eng.memset(xs[:, :, W - H:W], 0.0), writes=[xb])
                    R.dma("sp", lambda eng: eng.dma_start(out=xs[:, :, 0:W - H], in_=src_v[:, :, t0 - H:S]),
                          f"xs{i}", writes=[xb])
                else:
                    R.dma("sp", lambda eng: eng.dma_start(out=xs[:], in_=src_v[:, :, t0 - H:t0 + 512 + H]),
                          f"xs{i}", writes=[xb])

            xs, xb = norm_subtile(R, N, loader, g_sb, hn, hn_b, g_b)
            for c in range(NC):
                g = c // 2
                w = POOL_W[g]
                si = ksc % NSC
                ksc += 1
                ws, wb = wsc[si], wsc_b[si]
                R.op("dve", lambda eng, ws=ws, c=c: eng.tensor_tensor(
                    out=ws[0][:, 1:W], in0=hn[:, c, 0:W - 1], in1=hn[:, c, 1:W], op=ALU.add),
                    reads=[hn_b], writes=[wb[0]])
                if g >= 1:
                    R.op("dve", lambda eng, ws=ws: eng.tensor_tensor(
                        out=ws[1][:, 2:W - 1], in0=ws[0][:, 1:W - 2], in1=ws[0][:, 3:W], op=ALU.add),
                        reads=[wb[0]], writes=[wb[1]])
                if g >= 2:
                    R.op("dve", lambda eng, ws=ws: eng.tensor_tensor(
                        out=ws[2][:, 4:W - 3], in0=ws[1][:, 2:W - 5], in1=ws[1][:, 6:W - 1], op=ALU.add),
                        reads=[wb[1]], writes=[wb[2]])
                if g >= 3:
                    R.op("dve", lambda eng, ws=ws: eng.tensor_tensor(
                        out=ws[3][:, 8:W - 8], in0=ws[2][:, 4:W - 12], in1=ws[2][:, 12:W - 4], op=ALU.add),
                        reads=[wb[2]], writes=[wb[3]])
                wsum, wsum_b = ws[g], wb[g]
                R.op("dve", lambda eng, wsum=wsum, c=c, w=w: eng.scalar_tensor_tensor(
                    out=pl[:, c, :], in0=wsum[:, H:H + 512], scalar=1.0 / w, in1=hn[:, c, H:H + 512],
                    op0=ALU.mult, op1=ALU.subtract),
                    reads=[wsum_b, hn_b], writes=[pl_b[c]])
                for side, (do, so) in ((0, (0, H)), (1, (504, H + 504))):
                    if (side == 0 and it == 0) or (side == 1 and it == NT - 1):
                        R.op("dve", lambda eng, si=si, wsum=wsum, g=g, side=side, so=so: eng.tensor_tensor(
                            out=t8[si][:], in0=wsum[:, so:so + 8], in1=pi_sb[:, g, side, :], op=ALU.mult),
                            reads=[wsum_b, pi_b], writes=[t8_b[si]])
                        R.op("dve", lambda eng, si=si, c=c, do=do, so=so: eng.tensor_tensor(
                            out=pl[:, c, do:do + 8], in0=t8[si][:], in1=hn[:, c, so:so + 8], op=ALU.subtract),
                            reads=[t8_b[si], hn_b], writes=[pl_b[c]])
            oi = it % 2
            for co in range(NC):
                g, mo = co // 2, co % 2
                yi = ky % 2
                ky += 1
                mm_group(R, yps[yi][:], [(wpb[:, g, kc, mo * 128:(mo + 1) * 128], pl[:, 2 * g + kc, :])
                                         for kc in range(2)],
                         reads=[wp_b, pl_b[2 * g], pl_b[2 * g + 1]], bank=yps_b[yi])
                R.op("act", lambda eng, yi=yi, co=co, xs=xs: eng.activation(
                    out=t1[yi][:], in_=xs[:, co, H:H + 512], func=AF.Identity, bias=bs_sb[:, co:co + 1]),
                    reads=[xb, bs_b], writes=[t1_b[yi]])
                R.op("dve", lambda eng, yi=yi, co=co, oi=oi: eng.scalar_tensor_tensor(
                    out=ot[oi][:, co, :], in0=yps[yi][:], scalar=s_sb[:, co:co + 1], in1=t1[yi][:],
                    op0=ALU.mult, op1=ALU.add),
                    reads=[yps_b[yi], t1_b[yi], s_b], writes=[ot_b[oi]])
            stores.append(R.dma("sp", lambda eng, oi=oi, t0=t0: eng.dma_start(
                out=dst_v[:, :, t0:t0 + 512], in_=ot[oi][:]), f"ot{oi}", reads=[ot_b[oi]]))
        last = {}
        for o in stores:
            last[id(o.sem)] = o
        R.emit(final_waits=list(last.values()))


def _evac(R, k, out_ap, in_ap, reads, writes):
    if k % 2 == 0:
        R.op("act", lambda eng: eng.activation(out=out_ap, in_=in_ap, func=AF.Copy), reads=reads, writes=writes)
    else:
        R.op("dve", lambda eng: eng.tensor_copy(out=out_ap, in_=in_ap), reads=reads, writes=writes)


def fourier_phase(nc, tag, src, dst, wf, bf, gain, ccsc, cstab, fscr):
    src_v = src.rearrange("(c p) t -> p c t", p=128)
    dst_v = dst.rearrange("(c p) t -> p c t", p=128)
    f_v = fscr.rearrange("(c p) t -> p c t", p=128)
    NSC = S // 128
    with ExitStack() as es0:
        AB = es0.enter_context(nc.sbuf_tensor(f"{tag}_AB", [128, NSC, 4, 512], BF16))
        with ExitStack() as es:
            t1 = tag + "a"
            R = Rec(nc, es, t1)
            sb = lambda name, shape, dt: es.enter_context(nc.sbuf_tensor(f"{t1}_{name}", shape, dt))
            N = NormCtx(nc, es, t1, nslots=1)
            g_sb = sb("gain", [128, NC], F32); g_b = Buf("gain")
            cc = sb("cc", [128, 2, 512], BF16); cc_b = Buf("cc")
            hT = [sb(f"hT{i}", [128, NC, 512], BF16) for i in range(2)]
            hT_b = [Buf(f"hT{i}") for i in range(2)]
            abp = [es.enter_context(nc.psum_tensor(f"{t1}_abp{i}", [128, 512], F32)) for i in range(4)]
            abp_b = [Buf(f"abp{i}") for i in range(4)]
            AB_b = Buf("AB")
            norm_init(R, N)
            R.dma("sp", lambda eng: eng.dma_start(out=g_sb[:], in_=gain), "c0", writes=[g_b])
            R.dma("pool", lambda eng: eng.dma_start(out=cc[:].rearrange("p k n -> p (k n)"),
                                                    in_=ccsc.rearrange("p k n -> p (k n)")), "c1", writes=[cc_b])
            kp = 0
            last_ev = []
            for it in range(S // 512):
                hi = it % 2
                norm_subtile(R, N, std_loader(src_v, it * 512), g_sb, hT[hi], hT_b[hi], g_b)
                for scl in range(4):
                    si = it * 4 + scl
                    for g in range(4):
                        pi_ = kp % 4
                        kp += 1
                        mm_group(R, abp[pi_][:], [(hT[hi][:, 2 * g + kc, scl * 128:(scl + 1) * 128], cc[:, kc, :])
                                                  for kc in range(2)],
                                 reads=[hT_b[hi], cc_b], bank=abp_b[pi_])
                        _evac(R, kp, AB[:, si, g, :], abp[pi_][:], [abp_b[pi_]], [AB_b])
            R.emit()
        with ExitStack() as es:
            t2 = tag + "b"
            R = Rec(nc, es, t2)
            sb = lambda name, shape, dt: es.enter_context(nc.sbuf_tensor(f"{t2}_{name}", shape, dt))
            NTB = 3
            tb = [sb(f"tb{i}", [128, 2, 4, 512], BF16) for i in range(NTB)]
            tb_b = [Buf(f"tb{i}") for i in range(NTB)]
            fo = [sb(f"fo{i}", [128, NC, 512], BF16) for i in range(2)]
            fo_b = [Buf(f"fo{i}") for i in range(2)]
            acc = [es.enter_context(nc.psum_tensor(f"{t2}_acc{i}", [128, 512], F32)) for i in range(8)]
            acc_b = [Buf(f"acc{i}") for i in range(8)]
            stores = []
            kt_ = 0
            for kt in range(8):
                for blk in range(8):
                    bi = kt_ % NTB
                    kt_ += 1
                    R.dma("sp", lambda eng, bi=bi, kt=kt, blk=blk: eng.dma_start(
                        out=tb[bi][:].rearrange("p a s k -> p (a s k)"),
                        in_=cstab[kt, blk].rearrange("p a s k -> p (a s k)")), f"tb{bi}", writes=[tb_b[bi]])
                    for sc in range(4):
                        si = blk * 4 + sc
                        for co in range(8):
                            g, half = co // 2, co % 2
                            R.op("pe", lambda eng, co=co, si=si, g=g, half=half, bi=bi, sc=sc: eng.matmul(
                                acc[co][:], AB[:, si, g, half * 128:(half + 1) * 128], tb[bi][:, 0, sc, :],
                                start=(si == 0), stop=False),
                                reads=[tb_b[bi]], writes=[acc_b[co]])
                            R.op("pe", lambda eng, co=co, si=si, g=g, half=half, bi=bi, sc=sc: eng.matmul(
                                acc[co][:], AB[:, si, g, 256 + half * 128:256 + (half + 1) * 128], tb[bi][:, 1, sc, :],
                                start=False, stop=(si == NSC - 1)),
                                reads=[tb_b[bi]], writes=[acc_b[co]])
                fi = kt % 2
                for co in range(8):
                    _evac(R, co, fo[fi][:, co, :], acc[co][:], [acc_b[co]], [fo_b[fi]])
                stores.append(R.dma("sp", lambda eng, fi=fi, kt=kt: eng.dma_start(
                    out=f_v[:, :, kt * 512:(kt + 1) * 512], in_=fo[fi][:]), f"fo{fi}", reads=[fo_b[fi]]))
            last = {}
            for o in stores:
                last[id(o.sem)] = o
            R.emit(final_waits=list(last.values()))
    with ExitStack() as es:
        t3 = tag + "c"
        R = Rec(nc, es, t3)
        sb = lambda name, shape, dt: es.enter_context(nc.sbuf_tensor(f"{t3}_{name}", shape, dt))
        wfb = sb("wf", [128, NC, 1024], BF16); wf_b = Buf("wf")
        b_sb = sb("bf", [128, NC], F32); b_b = Buf("bf")
        ft = [sb(f"ft{i}", [128, NC, 512], BF16) for i in range(2)]
        ft_b = [Buf(f"ft{i}") for i in range(2)]
        xt = [sb(f"xt{i}", [128, NC, 512], F32) for i in range(2)]
        xt_b = [Buf(f"xt{i}") for i in range(2)]
        ot = [sb(f"ot{i}", [128, NC, 512], F32) for i in range(2)]
        ot_b = [Buf(f"ot{i}") for i in range(2)]
        zps = [es.enter_context(nc.psum_tensor(f"{t3}_zps{i}", [128, 512], F32)) for i in range(2)]
        zps_b = [Buf(f"zps{i}") for i in range(2)]
        R.dma("sp", lambda eng: eng.dma_start(out=b_sb[:], in_=bf), "c0", writes=[b_b])
        for c in range(NC):
            R.dma("pool", lambda eng, c=c: eng.dma_start(out=wfb[:, c, :], in_=wf[:, c, :]), "c1", writes=[wf_b])
        stores = []
        kz = 0
        for it in range(S // 512):
            i = it % 2
            tsl = slice(it * 512, (it + 1) * 512)
            R.dma("sp", lambda eng, i=i, tsl=tsl: eng.dma_start(out=ft[i][:], in_=f_v[:, :, tsl]), f"ft{i}", writes=[ft_b[i]])
            R.dma("sp", lambda eng, i=i, tsl=tsl: eng.dma_start(out=xt[i][:], in_=src_v[:, :, tsl]), f"xt{i}", writes=[xt_b[i]])
            for m in range(NC):
                zi = kz % 2
                kz += 1
                mm_group(R, zps[zi][:], [(wfb[:, c, m * 128:(m + 1) * 128], ft[i][:, c, :]) for c in range(NC)],
                         reads=[wf_b, ft_b[i]], bank=zps_b[zi])
                R.op("dve", lambda eng, zi=zi, m=m, i=i: eng.scalar_tensor_tensor(
                    out=ot[i][:, m, :], in0=zps[zi][:], scalar=b_sb[:, m:m + 1], in1=xt[i][:, m, :],
                    op0=ALU.add, op1=ALU.add),
                    reads=[zps_b[zi], b_b, xt_b[i]], writes=[ot_b[i]])
            stores.append(R.dma("sp", lambda eng, i=i, tsl=tsl: eng.dma_start(
                out=dst_v[:, :, tsl], in_=ot[i][:]), f"ot{i}", reads=[ot_b[i]]))
        last = {}
        for o in stores:
            last[id(o.sem)] = o
        R.emit(final_waits=list(last.values()))


HD = 128
NQH = 8
NKVH = 2


def attn_phase(nc, tag, src, dst, wqkv, wo, qkg, gain, rotm, cstab):
    src_v = src.rearrange("(c p) t -> p c t", p=128)
    dst_v = dst.rearrange("(c p) t -> p c t", p=128)
    NT = S // 512
    NKC = S // 128

    class Ctx:
        pass

    def common(R, es, t, need_q):
        C = Ctx()
        sb = lambda name, shape, dt: es.enter_context(nc.sbuf_tensor(f"{t}_{name}", shape, dt))
        ps = lambda name: es.enter_context(nc.psum_tensor(f"{t}_{name}", [128, 512], F32))
        C.sb, C.ps = sb, ps
        C.SS = ps("ss"); C.SS_b = Buf("ss")
        C.N = NormCtx(nc, es, t, nslots=1, ss=C.SS, ss_b=C.SS_b)
        C.A = [ps(f"A{i}") for i in range(2)]; C.A_b = [Buf(f"A{i}") for i in range(2)]
        C.ROT = ps("rot"); C.ROT_b = Buf("rot")
        C.g_sb = sb("gain", [128, NC], F32); C.g_b = Buf("gain")
        C.qkg_sb = sb("qkg", [128, 2], F32); C.qkg_b = Buf("qkg")
        C.rot_sb = sb("rotm", [128, 128], BF16); C.rotm_b = Buf("rotm")
        C.hT = [sb(f"hT{i}", [128, NC, 512], BF16) for i in range(2)]; C.hT_b = [Buf(f"hT{i}") for i in range(2)]
        C.cs = [sb(f"cs{i}", [128, 2, 512], F32) for i in range(2)]; C.cs_b = [Buf(f"cs{i}") for i in range(2)]
        C.sqh = [sb(f"sqh{i}", [128, 512], BF16) for i in range(2)]; C.sqh_b = [Buf(f"sqh{i}") for i in range(2)]
        C.qg = [sb(f"qg{i}", [128, 512], BF16) for i in range(2)]; C.qg_b = [Buf(f"qg{i}") for i in range(2)]
        C.rs = [sb(f"rs{i}", [128, 512], F32) for i in range(2)]; C.rs_b = [Buf(f"rs{i}") for i in range(2)]
        C.ta = [sb(f"ta{i}", [128, 512], F32) for i in range(2)]; C.ta_b = [Buf(f"ta{i}") for i in range(2)]
        C.tb = [sb(f"tb{i}", [128, 512], F32) for i in range(2)]; C.tb_b = [Buf(f"tb{i}") for i in range(2)]
        C.na = 0
        C.nn = 0
        norm_init(R, C.N)
        R.dma("sp", lambda eng: eng.dma_start(out=C.g_sb[:], in_=gain), "c0", writes=[C.g_b])
        R.dma("sp", lambda eng: eng.dma_start(out=C.qkg_sb[:], in_=qkg), "c1", writes=[C.qkg_b])
        R.dma("pool", lambda eng: eng.dma_start(out=C.rot_sb[:], in_=rotm), "c2", writes=[C.rotm_b])
        return C

    def rope_p1(R, C, src_ps, src_b, gcol):
        i = C.nn % 2
        C.nn += 1
        R.op("act", lambda eng: eng.activation(out=C.sqh[i][:], in_=src_ps, func=AF.Square),
             reads=[src_b], writes=[C.sqh_b[i]])
        R.op("act", lambda eng: eng.activation(out=C.qg[i][:], in_=src_ps, func=AF.Identity,
                                               scale=C.qkg_sb[:, gcol:gcol + 1]),
             reads=[src_b, C.qkg_b], writes=[C.qg_b[i]])
        return i

    def rope_p2(R, C, i, is_q, ci, out_ap, out_b):
        N = C.N
        R.op("pe", lambda eng: eng.matmul(C.SS[:], N.ones[:], C.sqh[i][:], start=True, stop=True),
             reads=[C.sqh_b[i], N.ones_b], writes=[C.SS_b])
        R.op("pe", lambda eng: eng.matmul(C.ROT[:], C.rot_sb[:], C.qg[i][:], start=True, stop=True),
             reads=[C.qg_b[i], C.rotm_b], writes=[C.ROT_b])
        a, b = (1.0, HD * EPS) if is_q else (1.0 / HD, EPS)
        R.op("dve", lambda eng: eng.tensor_scalar(out=C.rs[i][:], in0=C.SS[:], scalar1=a, scalar2=b,
                                                  op0=ALU.mult, op1=ALU.add),
             reads=[C.SS_b], writes=[C.rs_b[i]])
        R.op("act", lambda eng: eng.activation(out=C.rs[i][:], in_=C.rs[i][:], func=AF.Sqrt),
             reads=[C.rs_b[i]], writes=[C.rs_b[i]])
        R.op("dve", lambda eng: eng.reciprocal(out=C.rs[i][:], in_=C.rs[i][:]),
             reads=[C.rs_b[i]], writes=[C.rs_b[i]])
        R.op("pool", lambda eng: eng.tensor_tensor(out=C.ta[i][:], in0=C.qg[i][:], in1=C.cs[ci][:, 0, :], op=ALU.mult),
             reads=[C.qg_b[i], C.cs_b[ci]], writes=[C.ta_b[i]])
        R.op("dve", lambda eng: eng.tensor_tensor(out=C.tb[i][:], in0=C.ROT[:], in1=C.cs[ci][:, 1, :], op=ALU.mult),
             reads=[C.ROT_b, C.cs_b[ci]], writes=[C.tb_b[i]])
        R.op("pool", lambda eng: eng.tensor_tensor(out=C.ta[i][:], in0=C.ta[i][:], in1=C.tb[i][:], op=ALU.add),
             reads=[C.ta_b[i], C.tb_b[i]], writes=[C.ta_b[i]])
        R.op("dve", lambda eng: eng.tensor_tensor(out=out_ap, in0=C.ta[i][:], in1=C.rs[i][:], op=ALU.mult),
             reads=[C.ta_b[i], C.rs_b[i]], writes=[out_b])

    def norm_tile_thunks(R, C, it):
        bi = it % 2
        tsl = slice(it * 512, (it + 1) * 512)
        th = [lambda: R.dma("sp", lambda eng: eng.dma_start(out=C.cs[bi][:], in_=cstab[:, :, tsl]),
                            f"cs{bi}", writes=[C.cs_b[bi]])]
        th += norm_thunks(R, C.N, std_loader(src_v, it * 512), C.g_sb, C.hT[bi], C.hT_b[bi], C.g_b)
        return th

    with ExitStack() as es0:
        KT = es0.enter_context(nc.sbuf_tensor(f"{tag}_KT", [128, NKVH, S], BF16))
        V = es0.enter_context(nc.sbuf_tensor(f"{tag}_V", [128, NKC, 256], BF16))
        with ExitStack() as es:
            t = tag + "a"
            R = Rec(nc, es, t)
            C = common(R, es, t, False)
            wkv = C.sb("wkv", [128, NC, 512], BF16); wkv_b = Buf("wkv")
            KT_b = Buf("KT"); V_b = Buf("V")
            for c in range(NC):
                R.dma("pool", lambda eng, c=c: eng.dma_start(out=wkv[:, c, :], in_=wqkv[:, c, 1024:1536]),
                      "c3", writes=[wkv_b])
            for th in norm_tile_thunks(R, C, 0):
                th()
            for it in range(NT):
                bi = it % 2
                tsl = slice(it * 512, (it + 1) * 512)
                if it + 1 < NT:
                    for th in norm_tile_thunks(R, C, it + 1):
                        th()
                hT, hT_b = C.hT[bi], C.hT_b[bi]
                idx = []
                for kvh in range(NKVH):
                    ai = C.na % 2
                    C.na += 1
                    mm_group(R, C.A[ai][:], [(wkv[:, c, kvh * 128:(kvh + 1) * 128], hT[:, c, :]) for c in range(NC)],
                             reads=[wkv_b, hT_b], bank=C.A_b[ai])
                    idx.append(rope_p1(R, C, C.A[ai][:], C.A_b[ai], 1))
                for kvh in range(NKVH):
                    rope_p2(R, C, idx[kvh], False, bi, KT[:, kvh, tsl], KT_b)
                for scl in range(4):
                    ai = C.na % 2
                    C.na += 1
                    mm_group(R, C.A[ai][:, 0:256], [(hT[:, c, scl * 128:(scl + 1) * 128], wkv[:, c, 256:512])
                                                    for c in range(NC)],
                             reads=[wkv_b, hT_b], bank=C.A_b[ai])
                    _evac(R, scl, V[:, it * 4 + scl, :], C.A[ai][:, 0:256], [C.A_b[ai]], [V_b])
            R.emit()
        with ExitStack() as es:
            t = tag + "b"
            R = Rec(nc, es, t)
            C = common(R, es, t, True)
            sb, ps = C.sb, C.ps
            SC = [ps(f"sc{i}") for i in range(2)]; SC_b = [Buf(f"sc{i}") for i in range(2)]
            OACC = ps("oacc"); OACC_b = Buf("oacc")
            DEN = ps("den"); DEN_b = Buf("den")
            wq = sb("wq", [128, NC, 1024], BF16); wq_b = Buf("wq")
            wob = sb("wo", [128, NQH, 1024], BF16); wo_b = Buf("wo")
            QT = [sb(f"QT{i}", [128, NQH, 512], BF16) for i in range(2)]
            QT_b = [[Buf(f"QT{i}_{h}") for h in range(NQH)] for i in range(2)]
            NPT = 4
            PT = [sb(f"PT{i}", [128, 512], BF16) for i in range(NPT)]; PT_b = [Buf(f"PT{i}") for i in range(NPT)]
            oc = [sb(f"oc{i}", [128, 512], F32) for i in range(2)]; oc_b = [Buf(f"oc{i}") for i in range(2)]
            dn = [sb(f"dn{i}", [128, 512], F32) for i in range(2)]; dn_b = [Buf(f"dn{i}") for i in range(2)]
            OT = sb("OT", [128, NQH, 512], BF16); OT_b = [Buf(f"OT{h}") for h in range(NQH)]
            NX = 2
            xr = [sb(f"xr{i}", [128, 512], F32) for i in range(NX)]; xr_b = [Buf(f"xr{i}") for i in range(NX)]
            ot = [sb(f"ot{i}", [128, 512], F32) for i in range(NX)]; ot_b = [Buf(f"ot{i}") for i in range(NX)]
            for c in range(NC):
                R.dma("pool", lambda eng, c=c: eng.dma_start(out=wq[:, c, :], in_=wqkv[:, c, 0:1024]), "c4", writes=[wq_b])
            for c in range(NC):
                R.dma("pool", lambda eng, c=c: eng.dma_start(out=wob[:, c, :], in_=wo[:, c, :]), "c5", writes=[wo_b])

            def prologue_thunks(qt):
                bi = qt % 2
                th = norm_tile_thunks(R, C, qt)

                def pair(hp):
                    idx = []
                    for h in (2 * hp, 2 * hp + 1):
                        ai = C.na % 2
                        C.na += 1
                        mm_group(R, C.A[ai][:], [(wq[:, c, h * 128:(h + 1) * 128], C.hT[bi][:, c, :]) for c in range(NC)],
                                 reads=[wq_b, C.hT_b[bi]], bank=C.A_b[ai])
                        idx.append(rope_p1(R, C, C.A[ai][:], C.A_b[ai], 0))
                    for k, h in enumerate((2 * hp, 2 * hp + 1)):
                        rope_p2(R, C, idx[k], True, bi, QT[bi][:, h, :], QT_b[bi][h])
                for hp in range(NQH // 2):
                    th.append(lambda hp=hp: pair(hp))
                return th

            for th in prologue_thunks(0):
                th()
            stores = []
            kpt = 0
            ksc = 0
            ko = 0
            kx = 0
            for qt in range(NT):
                bi = qt % 2
                pending = prologue_thunks(qt + 1) if qt + 1 < NT else []
                npend = len(pending)
                for h in range(NQH):
                    kvh = h // 4

                    def qk(kc, h=h, kvh=kvh, bi=bi):
                        nonlocal ksc
                        si = ksc % 2
                        ksc += 1
                        R.op("pe", lambda eng, si=si, kc=kc, bi=bi, h=h, kvh=kvh: eng.matmul(
                            SC[si][:], KT[:, kvh, kc * 128:(kc + 1) * 128], QT[bi][:, h, :], start=True, stop=True),
                            reads=[QT_b[bi][h]], writes=[SC_b[si]])
                        return si

                    def ex(si):
                        nonlocal kpt
                        pi_ = kpt % NPT
                        kpt += 1
                        R.op("act", lambda eng, si=si, pi_=pi_: eng.activation(out=PT[pi_][:], in_=SC[si][:], func=AF.Exp),
                             reads=[SC_b[si]], writes=[PT_b[pi_]])
                        return pi_

                    def pv(kc, pi_, kvh=kvh):
                        R.op("pe", lambda eng, kc=kc, pi_=pi_: eng.matmul(
                            OACC[:], V[:, kc, kvh * 128:(kvh + 1) * 128], PT[pi_][:],
                            start=(kc == 0), stop=(kc == NKC - 1)),
                            reads=[PT_b[pi_]], writes=[OACC_b])
                        R.op("pe", lambda eng, kc=kc, pi_=pi_: eng.matmul(
                            DEN[:], C.N.ones[:], PT[pi_][:], start=(kc == 0), stop=(kc == NKC - 1)),
                            reads=[PT_b[pi_]], writes=[DEN_b])

                    s0 = qk(0)
                    p_prev = ex(s0)
                    for kc in range(1, NKC):
                        s1 = qk(kc)
                        pv(kc - 1, p_prev)
                        p_prev = ex(s1)
                        if pending and kc % 2 == 0 and (len(pending) > NQH // 2 or kc == 16):
                            if h < 6:
                                pending.pop(0)()
                    pv(NKC - 1, p_prev)
                    oi = ko % 2
                    ko += 1
                    R.op("act", lambda eng, oi=oi: eng.activation(out=oc[oi][:], in_=OACC[:], func=AF.Copy),
                         reads=[OACC_b], writes=[oc_b[oi]])
                    R.op("dve", lambda eng, oi=oi: eng.tensor_copy(out=dn[oi][:], in_=DEN[:]),
                         reads=[DEN_b], writes=[dn_b[oi]])
                    R.op("dve", lambda eng, oi=oi: eng.reciprocal(out=dn[oi][:], in_=dn[oi][:]),
                         reads=[dn_b[oi]], writes=[dn_b[oi]])
                    R.op("dve", lambda eng, oi=oi, h=h: eng.tensor_tensor(out=OT[:, h, :], in0=oc[oi][:], in1=dn[oi][:],
                                                                         op=ALU.mult),
                         reads=[oc_b[oi], dn_b[oi]], writes=[OT_b[h]])
                while pending:
                    pending.pop(0)()
                for m in range(NC):
                    ai = C.na % 2
                    C.na += 1
                    xi = kx % NX
                    kx += 1
                    t0 = qt * 512
                    R.dma("sp", lambda eng, xi=xi, m=m, t0=t0: eng.dma_start(
                        out=xr[xi][:], in_=src_v[:, m, t0:t0 + 512]), f"xr{xi}", writes=[xr_b[xi]])
                    mm_group(R, C.A[ai][:], [(wob[:, h, m * 128:(m + 1) * 128], OT[:, h, :]) for h in range(NQH)],
                             reads=[wo_b] + OT_b, bank=C.A_b[ai])
                    R.op("dve", lambda eng, ai=ai, xi=xi: eng.tensor_tensor(
                        out=ot[xi][:], in0=C.A[ai][:], in1=xr[xi][:], op=ALU.add),
                        reads=[C.A_b[ai], xr_b[xi]], writes=[ot_b[xi]])
                    stores.append(R.dma("sp", lambda eng, xi=xi, m=m, t0=t0: eng.dma_start(
                        out=dst_v[:, m, t0:t0 + 512], in_=ot[xi][:]), f"ot{xi}", reads=[ot_b[xi]]))
            last = {}
            for o in stores:
                last[id(o.sem)] = o
            R.emit(final_waits=list(last.values()))


def build_program(phases):
    nc = bass.Bass("TRN2", target_bir_lowering=False)
    ext = lambda name, shape: nc.dram_tensor(name, shape, F32, kind="ExternalInput").ap()
    xT = ext("xT", [D, S])
    yT = nc.dram_tensor("yT", [D, S], F32, kind="ExternalOutput").ap()
    wts = {}
    for f in ("ffn1", "ffn2"):
        wts[f + "_wg"] = ext(f + "_wg", [DEPTH, NJ, 128, NC, 128])
        wts[f + "_wu"] = ext(f + "_wu", [DEPTH, NJ, 128, NC, 128])
        wts[f + "_wd"] = ext(f + "_wd", [DEPTH, NC, 128, NJ, 128])
        wts[f + "_g"] = ext(f + "_g", [DEPTH, 128, NC])
    wts["mix_g"] = ext("mix_g", [DEPTH, 128, NC])
    wts["fin_g"] = ext("fin_g", [128, NC])
    wts["pool_w"] = ext("pool_w", [2, 128, 4, 2, 256])
    wts["pool_b"] = ext("pool_b", [2, 128, NC])
    wts["pool_s"] = ext("pool_s", [2, 128, NC])
    wts["pool_inv"] = ext("pool_inv", [128, 4, 2, 8])
    wts["f_w"] = ext("f_w", [128, NC, 1024])
    wts["f_b"] = ext("f_b", [128, NC])
    wts["f_ccsc"] = ext("f_ccsc", [128, 2, 512])
    wts["f_cstab"] = nc.dram_tensor("f_cstab", [8, 8, 128, 2, 4, 512], BF16, kind="ExternalInput").ap()
    fscr = nc.dram_tensor("f_scr", [D, S], BF16).ap()
    wts["a_wqkv"] = ext("a_wqkv", [128, NC, 1536])
    wts["a_wo"] = ext("a_wo", [128, NQH, 1024])
    wts["a_qkg"] = ext("a_qkg", [128, 2])
    wts["a_rotm"] = ext("a_rotm", [128, 128])
    wts["a_cs"] = ext("a_cs", [128, 2, S])
    ra = nc.dram_tensor("res_a", [D, S], F32).ap()
    rb = nc.dram_tensor("res_b", [D, S], F32).ap()
    bufs = [ra, rb]
    cur = xT
    k = 0
    for pi, ph in enumerate(phases):
        last = pi == len(phases) - 1
        dst = yT if last else bufs[k % 2]
        kind = ph[0]
        tag = f"p{pi}"
        if kind == "ffn":
            _, f, layer = ph
            ffn_phase(nc, tag, cur, dst, wts[f + "_wg"][layer], wts[f + "_wu"][layer],
                      wts[f + "_wd"][layer], wts[f + "_g"][layer])
        elif kind == "pool":
            _, layer, j = ph
            pool_phase(nc, tag, cur, dst, wts["pool_w"][j], wts["pool_b"][j], wts["pool_s"][j],
                       wts["mix_g"][layer], wts["pool_inv"])
        elif kind == "fourier":
            _, layer, j = ph
            fourier_phase(nc, tag, cur, dst, wts["f_w"], wts["f_b"], wts["mix_g"][layer],
                          wts["f_ccsc"], wts["f_cstab"], fscr)
        elif kind == "attn":
            _, layer, j = ph
            attn_phase(nc, tag, cur, dst, wts["a_wqkv"], wts["a_wo"], wts["a_qkg"], wts["mix_g"][layer],
                       wts["a_rotm"], wts["a_cs"])
        elif kind == "final":
            final_norm_phase(nc, tag, cur, dst, wts["fin_g"])
        else:
            raise ValueError(kind)
        cur = dst
        k += 1
    return nc


def _pc(v):
    v = np.asarray(v, np.float32)
    return np.ascontiguousarray(np.swapaxes(v.reshape(v.shape[:-1] + (NC, 128)), -1, -2))


def const_tables():
    t = {}
    pinv = np.zeros((4, 2, 8), np.float32)
    for g, w in enumerate(POOL_W):
        for side in range(2):
            for i in range(8):
                tt = i if side == 0 else S - 8 + i
                lo = max(tt - w // 2, 0)
                hi = min(tt + w // 2, S)
                pinv[g, side, i] = 1.0 / float(hi - lo)
    t["pool_inv"] = np.ascontiguousarray(np.broadcast_to(pinv[None], (128, 4, 2, 8)))
    rows = S // 64
    row = np.repeat(np.arange(rows, dtype=np.float32), 64)
    col = np.tile(np.arange(64, dtype=np.float32), rows)
    inv_freq = (np.float32(10000.0) ** (-np.arange(0, 64, 2, dtype=np.float32) / np.float32(64))).astype(np.float32)
    ang_r = row[:, None] * inv_freq[None, :]
    ang_c = col[:, None] * inv_freq[None, :]
    ang = np.concatenate([ang_r, ang_r, ang_c, ang_c], axis=-1).astype(np.float32)
    t["a_cs"] = np.ascontiguousarray(np.stack([np.cos(ang).T, np.sin(ang).T], axis=1)).astype(np.float32)
    rm = np.zeros((128, 128), np.float32)
    for d in range(128):
        if (d % 64) < 32:
            rm[d + 32, d] = -1.0
        else:
            rm[d - 32, d] = 1.0
    t["a_rotm"] = rm
    ch = np.arange(256, dtype=np.int64)
    ang = 2.0 * np.pi * ((ch[:, None] * ch[None, :]) % 256).astype(np.float64) / 256.0
    cc = np.concatenate([np.cos(ang), -np.sin(ang)], axis=1) / 16.0
    t["f_ccsc"] = np.ascontiguousarray(cc.reshape(2, 128, 512).transpose(1, 0, 2)).astype(np.float32)
    n = np.arange(S, dtype=np.float64)
    ct = (np.cos(2.0 * np.pi * n / S) / 64.0).astype(np.float32)
    st = (np.sin(2.0 * np.pi * n / S) / 64.0).astype(np.float32)
    s_idx = np.arange(S, dtype=np.int64)
    prod = (s_idx[:, None] * s_idx[None, :]) % S
    tab = np.stack([ct[prod], st[prod]], axis=0).astype(BF16_NP)
    tab = tab.reshape(2, 8, 4, 128, 8, 512)
    t["f_cstab"] = np.ascontiguousarray(tab.transpose(4, 1, 3, 0, 2, 5))
    return t


def prep_inputs(inp):
    x = np.asarray(inp["x"], np.float32)
    com = dict(const_tables())
    for f in ("ffn1", "ffn2"):
        wg = np.asarray(inp[f + "_w_gate"], np.float32)
        wu = np.asarray(inp[f + "_w_up"], np.float32)
        wd = np.asarray(inp[f + "_w_down"], np.float32)
        com[f + "_wg"] = np.ascontiguousarray(wg.reshape(DEPTH, NC, 128, NJ, 128).transpose(0, 3, 2, 1, 4))
        com[f + "_wu"] = np.ascontiguousarray(wu.reshape(DEPTH, NC, 128, NJ, 128).transpose(0, 3, 2, 1, 4))
        com[f + "_wd"] = np.ascontiguousarray(wd.reshape(DEPTH, NJ, 128, NC, 128).transpose(0, 3, 2, 1, 4))
        com[f + "_g"] = _pc(inp[f + "_norm"])
    com["mix_g"] = _pc(inp["mixer_norm"])
    com["fin_g"] = _pc(inp["final_norm"])
    pw = np.asarray(inp["pool_w"], np.float32)
    com["pool_w"] = np.ascontiguousarray(pw.reshape(2, 4, 2, 128, 256).transpose(0, 3, 1, 2, 4))
    com["pool_b"] = _pc(np.asarray(inp["pool_b"], np.float32).reshape(2, D))
    com["pool_s"] = _pc(inp["pool_scale"])
    wqkv = np.asarray(inp["attn_w_qkv"], np.float32)[0]
    com["a_wqkv"] = np.ascontiguousarray(wqkv.reshape(NC, 128, 1536).transpose(1, 0, 2))
    wo = np.asarray(inp["attn_w_o"], np.float32)[0]
    com["a_wo"] = np.ascontiguousarray(wo.reshape(NQH, 128, 1024).transpose(1, 0, 2))
    com["a_qkg"] = np.ascontiguousarray(np.stack([np.asarray(inp["attn_q_norm"], np.float32)[0],
                                                  np.asarray(inp["attn_k_norm"], np.float32)[0]], axis=1))
    fw = np.asarray(inp["fourier_w"], np.float32)[0]
    com["f_w"] = np.ascontiguousarray(fw.reshape(NC, 128, 1024).transpose(1, 0, 2))
    com["f_b"] = _pc(np.asarray(inp["fourier_b"], np.float32)[0])
    maps = []
    for b in range(NB):
        m = dict(com)
        m["xT"] = np.ascontiguousarray(x[b].T)
        maps.append(m)
    return maps


def run_phases(inp, phases):
    nc = build_program(phases)
    maps = prep_inputs(inp)
    res = run_bass_kernel_spmd(nc, maps, core_ids=list(range(NB)))
    out = np.stack([np.ascontiguousarray(r["yT"].T) for r in res.results], axis=0)
    return out


def full_phases():
    phases = []
    for i in range(DEPTH):
        phases.append(("ffn", "ffn1", i))
        kind = i % 3
        if kind == 0:
            phases.append(("pool", i, i // 3))
        elif kind == 1:
            phases.append(("fourier", i, i // 3))
        else:
            phases.append(("attn", i, i // 3))
        phases.append(("ffn", "ffn2", i))
    phases.append(("final",))
    return phases


def kernel(**inputs):
    return run_phases(inputs, full_phases()).astype(np.float32)
```
